# Optimizing a Trainium2 kernel written in Bass

```python
import jax, jax.numpy as jnp
from jax import lax
import numpy as np

D_MODEL = 1024
BATCH = 16
SEQ = 2048
DEPTH = 1
DEC_BATCH = 8
DEC_SEQ = 64
PAST_LEN = 2048

CHUNK = 64
N_BAND_CHUNKS = 8
PAST_WINDOW = CHUNK * N_BAND_CHUNKS
BAND = PAST_WINDOW + CHUNK
MLP_CHUNK = 128
D_A = D_MODEL // 2
D_B = D_MODEL - D_A
A_GROUPS = 8
A_DG = D_A // A_GROUPS
B_HEADS = 8
B_HD = D_B // B_HEADS
REL_CLIP = 64
N_REL = 2 * REL_CLIP + 1
D_FF = ((8 * D_MODEL // 3 + 255) // 256) * 256
D_PLE = 256
D_IN = 2 * D_A + 3 * D_B
EPS = 1e-6
NEG_INF = -1e30

kernel_name = 'hymba_gmlp_chunkband_stream'


def rmsnorm(x, g):
    xf = x.astype(jnp.float32)
    y = xf * lax.rsqrt(jnp.mean(xf * xf, axis=-1, keepdims=True) + EPS)
    return (y * g.astype(jnp.float32)).astype(x.dtype)


def split_heads(z):
    b, s = z.shape[:2]
    u, v, q, k, vb = jnp.split(z, [D_A, 2 * D_A, 2 * D_A + D_B, 2 * D_A + 2 * D_B], axis=-1)
    u = jax.nn.gelu(u).reshape(b, s, A_GROUPS, A_DG)
    v = jax.nn.gelu(v).reshape(b, s, A_GROUPS, A_DG)
    q = q.reshape(b, s, B_HEADS, B_HD)
    k = k.reshape(b, s, B_HEADS, B_HD)
    vb = vb.reshape(b, s, B_HEADS, B_HD)
    return u, v, q, k, vb


def spatial_gate(u, v, w_s, b_s):
    b, s = u.shape[:2]
    t = w_s.shape[-1]
    w = jnp.where(jnp.tril(jnp.ones((t, t), dtype=bool)), w_s, 0)
    vc = v.reshape(b, s // t, t, A_GROUPS, A_DG)
    mixed = jnp.einsum('gts,bcsgd->bctgd', w, vc) + jnp.transpose(b_s)[None, None, :, :, None]
    return u * mixed.reshape(b, s, A_GROUPS, A_DG)


def rel_bias_lookup(rel_bias, rel):
    return rel_bias[:, jnp.clip(rel, -REL_CLIP, REL_CLIP) + REL_CLIP].astype(jnp.float32)


def attend(q, k, v, bias, valid):
    sc = jnp.einsum('bqhd,bkhd->bhqk', q, k).astype(jnp.float32) * (B_HD ** -0.5) + bias
    sc = jnp.where(valid, sc, NEG_INF)
    p = jax.nn.softmax(sc, axis=-1).astype(v.dtype)
    return jnp.einsum('bhqk,bkhd->bqhd', p, v)


def band_attention_prompt(q, k, v, rel_bias):
    b, s = q.shape[:2]
    nc = s // CHUNK
    pad = ((0, 0), (PAST_WINDOW, 0), (0, 0), (0, 0))
    kp = jnp.pad(k, pad)
    vp = jnp.pad(v, pad)
    qc = jnp.moveaxis(q.reshape(b, nc, CHUNK, B_HEADS, B_HD), 1, 0)
    rel = PAST_WINDOW + jnp.arange(CHUNK)[:, None] - jnp.arange(BAND)[None, :]
    bias = rel_bias_lookup(rel_bias, rel)[None]

    def one_chunk(args):
        c, q_blk = args
        start = c * CHUNK
        k_blk = lax.dynamic_slice_in_dim(kp, start, BAND, axis=1)
        v_blk = lax.dynamic_slice_in_dim(vp, start, BAND, axis=1)
        valid = (start - PAST_WINDOW + jnp.arange(BAND)) >= 0
        return attend(q_blk, k_blk, v_blk, bias, valid[None, None, None, :])

    out = lax.map(one_chunk, (jnp.arange(nc), qc))
    return jnp.moveaxis(out, 0, 1).reshape(b, s, D_B)


def band_attention_sample(q, k, v, k_cache, v_cache, rel_bias):
    b, n = q.shape[:2]
    r = k_cache.shape[1]
    kk = jnp.concatenate([k_cache.astype(k.dtype), k], axis=1)
    vv = jnp.concatenate([v_cache.astype(v.dtype), v], axis=1)
    k_off = jnp.concatenate([jnp.arange(r) - r, jnp.arange(n)])
    rel = jnp.arange(n)[:, None] - k_off[None, :]
    bias = rel_bias_lookup(rel_bias, rel)[None]
    valid = jnp.ones((1, 1, 1, r + n), dtype=bool)
    return attend(q, kk, vv, bias, valid).reshape(b, n, D_B)


def mixer_out(ya, yb, g_out_a, g_out_b, w_out):
    b, s = ya.shape[:2]
    ya = rmsnorm(ya.reshape(b, s, D_A), g_out_a)
    yb = rmsnorm(yb, g_out_b)
    return jnp.concatenate([ya, yb], axis=-1) @ w_out


def channel_and_ple(h, p_l, g_ffn, w_gate, w_up, w_down, g_ple, w_ple_gate, w_ple_proj):
    f = rmsnorm(h, g_ffn)
    h = h + (jax.nn.silu(f @ w_gate) * (f @ w_up)) @ w_down
    gate = jax.nn.sigmoid(rmsnorm(h, g_ple) @ w_ple_gate)
    return h + gate * (p_l.astype(h.dtype) @ w_ple_proj)


def setup_inputs(seed: int = 0) -> dict:
    key = jax.random.key(seed)
    ks = jax.random.split(key, 24)
    f32 = jnp.float32
    r = min(PAST_WINDOW, PAST_LEN)

    def nrm(k, shape, scale):
        return jax.random.normal(k, shape, f32) * scale

    def gain(k, shape):
        return 1.0 + 0.01 * jax.random.normal(k, shape, f32)

    return {
        'x_prompt': nrm(ks[0], (BATCH, SEQ, D_MODEL), 1.0),
        'x_sample': nrm(ks[1], (DEC_BATCH, DEC_SEQ, D_MODEL), 1.0),
        'p_prompt': nrm(ks[2], (DEPTH, BATCH, SEQ, D_PLE), 1.0),
        'p_sample': nrm(ks[3], (DEPTH, DEC_BATCH, DEC_SEQ, D_PLE), 1.0),
        'cache_band_k': nrm(ks[4], (DEPTH, DEC_BATCH, r, B_HEADS, B_HD), 1.0),
        'cache_band_v': nrm(ks[5], (DEPTH, DEC_BATCH, r, B_HEADS, B_HD), 1.0),
        'g_attn': gain(ks[6], (DEPTH, D_MODEL)),
        'w_in': nrm(ks[7], (DEPTH, D_MODEL, D_IN), D_MODEL ** -0.5),
        'g_v': gain(ks[8], (DEPTH, A_GROUPS, A_DG)),
        'w_spatial': nrm(ks[9], (DEPTH, A_GROUPS, MLP_CHUNK, MLP_CHUNK), 0.5 * MLP_CHUNK ** -0.5),
        'b_spatial': gain(ks[10], (DEPTH, A_GROUPS, MLP_CHUNK)),
        'rel_bias': nrm(ks[11], (DEPTH, B_HEADS, N_REL), 0.1),
        'g_out_a': gain(ks[12], (DEPTH, D_A)),
        'g_out_b': gain(ks[13], (DEPTH, D_B)),
        'w_out': nrm(ks[14], (DEPTH, D_MODEL, D_MODEL), D_MODEL ** -0.5),
        'g_ffn': gain(ks[15], (DEPTH, D_MODEL)),
        'w_gate': nrm(ks[16], (DEPTH, D_MODEL, D_FF), D_MODEL ** -0.5),
        'w_up': nrm(ks[17], (DEPTH, D_MODEL, D_FF), D_MODEL ** -0.5),
        'w_down': nrm(ks[18], (DEPTH, D_FF, D_MODEL), D_FF ** -0.5),
        'g_ple': gain(ks[19], (DEPTH, D_MODEL)),
        'w_ple_gate': nrm(ks[20], (DEPTH, D_MODEL, D_MODEL), D_MODEL ** -0.5),
        'w_ple_proj': nrm(ks[21], (DEPTH, D_PLE, D_MODEL), D_PLE ** -0.5),
        'g_final': gain(ks[22], (D_MODEL,)),
    }


def reference(x_prompt, x_sample, p_prompt, p_sample, cache_band_k, cache_band_v,
              g_attn, w_in, g_v, w_spatial, b_spatial, rel_bias, g_out_a, g_out_b, w_out,
              g_ffn, w_gate, w_up, w_down, g_ple, w_ple_gate, w_ple_proj, g_final):
    hp, hs = x_prompt, x_sample
    n = x_sample.shape[1]
    r_p = min(PAST_WINDOW, x_prompt.shape[1])
    kp_l, vp_l, ks_l, vs_l, va_l = [], [], [], [], []
    for l in range(DEPTH):
        up, vp, qp, kp, vbp = split_heads(rmsnorm(hp, g_attn[l]) @ w_in[l])
        vp = rmsnorm(vp, g_v[l])
        ya = spatial_gate(up, vp, w_spatial[l], b_spatial[l])
        yb = band_attention_prompt(qp, kp, vbp, rel_bias[l])
        hp = hp + mixer_out(ya, yb, g_out_a[l], g_out_b[l], w_out[l])
        hp = channel_and_ple(hp, p_prompt[l], g_ffn[l], w_gate[l], w_up[l], w_down[l],
                             g_ple[l], w_ple_gate[l], w_ple_proj[l])
        kp_l.append(kp[:, -r_p:])
        vp_l.append(vbp[:, -r_p:])
        us, vs, qs, ks, vbs = split_heads(rmsnorm(hs, g_attn[l]) @ w_in[l])
        vs = rmsnorm(vs, g_v[l])
        ya = spatial_gate(us, vs, w_spatial[l][:, :n, :n], b_spatial[l][:, :n])
        yb = band_attention_sample(qs, ks, vbs, cache_band_k[l], cache_band_v[l], rel_bias[l])
        hs = hs + mixer_out(ya, yb, g_out_a[l], g_out_b[l], w_out[l])
        hs = channel_and_ple(hs, p_sample[l], g_ffn[l], w_gate[l], w_up[l], w_down[l],
                             g_ple[l], w_ple_gate[l], w_ple_proj[l])
        ks_l.append(ks)
        vs_l.append(vbs)
        va_l.append(vs)
    y_prompt = rmsnorm(hp, g_final)
    y_sample = rmsnorm(hs, g_final)
    return (y_prompt, y_sample, jnp.stack(kp_l), jnp.stack(vp_l), jnp.stack(ks_l), jnp.stack(vs_l), jnp.stack(va_l))
```

```python
import contextlib
import math

import numpy as np
import concourse.bass as bass
import concourse.mybir as mybir
from concourse.bass_utils import run_bass_kernel_spmd

F32 = mybir.dt.float32
BF16 = mybir.dt.bfloat16
AF = mybir.ActivationFunctionType
ALU = mybir.AluOpType
AX = mybir.AxisListType

D = 1024
DA = 512
DIN = 2560
DFF = 2816
NJ = DFF // 128
DPLE = 256
SEQ = 2048
NSEQ = 2
NS = 64
NCACHE = 512
T = 512
EPS = 1e-6
NCORES = 8
SLOT_ELEMS = 11264
NEG = -30000.0


class StopBuild(Exception):
    pass


class Trk:
    def __init__(self):
        self.w = {}
        self.r = {}
        self.prev = {}


def _merge(dst, src):
    for k, v in src.items():
        if dst.get(k, 0) < v:
            dst[k] = v


class TB(Trk):
    def __init__(self, t):
        super().__init__()
        self.t = t
        self.free = True

    def release(self):
        assert not self.free
        self.free = True


class Rot:
    def __init__(self, items, check=True):
        self.items = items
        self.i = 0
        self.check = check

    def next(self):
        n = len(self.items)
        if not self.check:
            it = self.items[self.i % n]
            self.i += 1
            return it
        for k in range(n):
            it = self.items[(self.i + k) % n]
            if it.free:
                it.free = False
                self.i = (self.i + k + 1) % n
                return it
        raise AssertionError("rotating pool exhausted: a buffer was not released")


class Eng:
    def __init__(self, name, eng, sem_idx):
        self.name = name
        self.e = eng
        self.sem = sem_idx
        self.cnt = 0
        self.known = {}


class Ctx:
    def __init__(self, nc, es):
        self.nc = nc
        self.es = es
        self.sems = []
        self.hist = {}
        self.pe = self._eng("pe", nc.tensor)
        self.act = self._eng("act", nc.scalar)
        self.dve = self._eng("dve", nc.vector)
        self.pool = self._eng("pool", nc.gpsimd)
        self.sp = self._eng("sp", nc.sync)
        self.dma_pools = {
            "sp": Rot([[self.new_sem("dsp%d" % i), 0] for i in range(16)], check=False),
            "pool": Rot([[self.new_sem("dpl%d" % i), 0] for i in range(8)], check=False),
        }

    def new_sem(self, name):
        h = self.es.enter_context(self.nc.semaphore(name))
        self.sems.append(h)
        return len(self.sems) - 1

    def _eng(self, name, eng):
        return Eng(name, eng, self.new_sem("s_" + name))

    def wait(self, E, deps, embed=False):
        need = sorted(((k, v) for k, v in deps.items() if E.known.get(k, 0) < v), key=lambda kv: kv[0])
        todo = []
        for k, v in need:
            if E.known.get(k, 0) >= v:
                continue
            todo.append((k, v))
            E.known[k] = v
            _merge(E.known, self.hist.get((k, v), {}))
        last = None
        if embed and todo:
            last = todo.pop()
        for k, v in todo:
            E.e.wait_ge(self.sems[k], v)
        return last

    def _deps(self, reads, writes):
        deps = {}
        for b in reads:
            _merge(deps, b.w)
        for b in writes:
            if b.r:
                b.prev = dict(b.r)
                _merge(b.prev, b.w)
                b.r = {}
                b.w = {}
            _merge(deps, b.prev)
        return deps

    def _post(self, tok, reads, writes):
        k, v = tok
        for b in reads:
            if b.r.get(k, 0) < v:
                b.r[k] = v
        for b in writes:
            if b.w.get(k, 0) < v:
                b.w[k] = v

    def op(self, E, fn, reads=(), writes=()):
        last = self.wait(E, self._deps(reads, writes), embed=True)
        ins = fn()
        if last is not None:
            ins._wait_ge(self.sems[last[0]], last[1])
        E.cnt += 1
        ins.then_inc(self.sems[E.sem], 1)
        self.hist[(E.sem, E.cnt)] = dict(E.known)
        self._post((E.sem, E.cnt), reads, writes)

    def group(self, E, fns, reads=(), writes=()):
        last = self.wait(E, self._deps(reads, writes), embed=True)
        ins = None
        for i, fn in enumerate(fns):
            ins = fn()
            if i == 0 and last is not None:
                ins._wait_ge(self.sems[last[0]], last[1])
        E.cnt += 1
        ins.then_inc(self.sems[E.sem], 1)
        self.hist[(E.sem, E.cnt)] = dict(E.known)
        self._post((E.sem, E.cnt), reads, writes)

    def dma(self, Q, out, in_, reads=(), writes=(), sem=None, **kw):
        deps = self._deps(reads, writes)
        if sem is None:
            sl = self.dma_pools[Q.name].next()
            if sl[1] > 0:
                _merge(deps, {sl[0]: sl[1]})
        else:
            sl = sem
        self.wait(Q, deps)
        ins = Q.e.dma_start(out=out, in_=in_, **kw)
        sl[1] += 16
        ins.then_inc(self.sems[sl[0]], 16)
        self.hist[(sl[0], sl[1])] = dict(Q.known)
        self._post((sl[0], sl[1]), reads, writes)


def build():
    env = {}
    DBG = int(env.get('KDBG', '99'))
    KS = int(env.get('KS', '99'))
    nc = bass.Bass("TRN2", target_bir_lowering=False)
    es = contextlib.ExitStack()
    cx = Ctx(nc, es)
    PE, ACT, DVE, POOL, SP = cx.pe, cx.act, cx.dve, cx.pool, cx.sp

    def din(name, shape):
        return nc.dram_tensor(name, list(shape), F32, kind="ExternalInput")

    def dout(name, shape):
        return nc.dram_tensor(name, list(shape), F32, kind="ExternalOutput")

    xp = din("xp", [NSEQ, SEQ, D])
    xs = din("xs", [NS, D])
    pp = din("pp", [NSEQ, SEQ, DPLE])
    ps = din("ps", [NS, DPLE])
    ck = din("ck", [NCACHE, DA])
    cv = din("cv", [NCACHE, DA])
    g_attn = din("g_attn", [D])
    w_in = din("w_in", [D, DIN])
    g_v = din("g_v", [DA])
    w_sp = din("w_sp", [8, 128, 128])
    b_sp = din("b_sp", [8, 128])
    relb = din("relb", [8, 129])
    g_oa = din("g_oa", [DA])
    g_ob = din("g_ob", [DA])
    w_out = din("w_out", [D, D])
    g_ffn = din("g_ffn", [D])
    w_gate = din("w_gate", [D, DFF])
    w_up = din("w_up", [D, DFF])
    w_down = din("w_down", [DFF, D])
    g_ple = din("g_ple", [D])
    w_pg = din("w_pg", [D, D])
    w_pp = din("w_pp", [DPLE, D])
    g_fin = din("g_fin", [D])

    yp = dout("yp", [NSEQ, SEQ, D])
    ys = dout("ys", [NS, D])
    nkp = dout("nkp", [NSEQ, T, DA])
    nvp = dout("nvp", [NSEQ, T, DA])
    nks = dout("nks", [NS, DA])
    nvs = dout("nvs", [NS, DA])
    nvas = dout("nvas", [NS, DA])
    out_trk = Trk()

    def sb(name, shape, dt):
        return TB(nc.alloc_sbuf_tensor(name, list(shape), dt))

    def ps_(name, shape, dt):
        return TB(nc.alloc_psum_tensor(name, list(shape), dt))

    pieces = []

    def add_piece(name, parts):
        off = 0
        views = []
        for (src, r0, nk, c0, ncol) in parts:
            views.append((off, nk, ncol))
            off += nk * ncol
        assert off <= SLOT_ELEMS, (name, off)
        dr = nc.dram_tensor("wsc_" + name, [128, off], BF16)
        pieces.append(dict(name=name, parts=parts, views=views, n=off, dram=dr, trk=Trk(),
                           sem=[cx.new_sem("cv_" + name), 0]))

    add_piece("uv", [(w_in, 0, 8, 0, 1024)])
    add_piece("vb", [(w_in, 0, 8, 2048, 512)])
    add_piece("qk", [(w_in, 0, 8, 1024, 1024)])
    add_piece("wo", [(w_out, 0, 8, 0, 1024)])
    JG = [(0, 5), (5, 5), (10, 5), (15, 5), (20, 2)]
    for gi, (j0, nj) in enumerate(JG):
        add_piece("gu%d" % gi, [(w_gate, 0, 8, j0 * 128, nj * 128), (w_up, 0, 8, j0 * 128, nj * 128)])
    add_piece("dn0", [(w_down, 0, NJ, 0, 512)])
    add_piece("dn1", [(w_down, 0, NJ, 512, 512)])
    add_piece("pg", [(w_pg, 0, 8, 0, 1024), (w_pp, 0, 2, 0, 1024)])
    piece_by_name = {p["name"]: p for p in pieces}
    tile_order = [p["name"] for p in pieces]

    ident = sb("ident", [128, 128], BF16)
    jmat = sb("jmat", [128, 128], BF16)
    gcol_attn = sb("gc_attn", [128, 8], F32)
    gcol_ffn = sb("gc_ffn", [128, 8], F32)
    gcol_ple = sb("gc_ple", [128, 8], F32)
    gcol_out = sb("gc_out", [128, 8], F32)
    gfin_bc = sb("gfin_bc", [128, D], F32)
    gv_bc = sb("gv_bc", [128, DA], F32)
    bs_t = sb("bs_t", [128, 8], F32)
    wsT = sb("wsT", [128, 8, 128], BF16)
    E43 = sb("E43", [128, 8, 256], BF16)
    E0 = sb("E0", [128, 8, 128], BF16)
    neghalf = sb("neghalf", [128, 8], F32)

    h_t = [sb("h%d" % s, [128, D], F32) for s in range(4)]
    kT = sb("kT", [128, 4, 1024], BF16)
    kT_half = [Trk(), Trk()]
    vaug = sb("vaug", [128, 8, 8, 65], BF16)
    va_half = [Trk(), Trk()]
    actT2 = [sb("actT%d" % i, [128, 8, T], BF16) for i in range(2)]
    xstage_rot = Rot([sb("xst%d" % i, [128, D], F32) for i in range(2)])
    pT = sb("pT", [128, 2, T], BF16)

    xn_rot = Rot([sb("xn%d" % i, [128, D], BF16) for i in range(5)])
    f512 = Rot([sb("f512_%d" % i, [128, 512], F32) for i in range(7)])
    vn_rot = Rot([sb("vn%d" % i, [128, DA], BF16) for i in range(4)])
    ynT_rot = Rot([sb("ynT%d" % i, [128, 8, 128], BF16) for i in range(2)])
    p_rot = Rot([sb("p%d" % i, [128, DPLE], F32) for i in range(4)])
    pb_rot = Rot([sb("pb%d" % i, [128, DPLE], BF16) for i in range(4)])
    stat_rot = Rot([sb("st%d" % i, [128, 8], F32) for i in range(24)], check=False)
    wslots = Rot([sb("wslot%d" % i, [128, SLOT_ELEMS], BF16) for i in range(3)])

    tp_rot = Rot([ps_("tp%d" % i, [128, 1024], BF16) for i in range(1)])
    tp = tp_rot.items[0]
    banks = Rot([ps_("bank%d" % i, [128, 512], F32) for i in range(7)])

    def act_fn(out, in_, func, **kw):
        return lambda: nc.scalar.activation(out, in_, func, **kw)

    def rstd_from(ss_ap, n, ncols, scale=None):
        ms = stat_rot.next()
        rs = stat_rot.next()
        return ms, rs

    def pool_rstd(ss, n, ncols=1, scale=None):
        ms = stat_rot.next()
        rs = stat_rot.next()
        if scale is None:
            cx.op(POOL, lambda: nc.gpsimd.tensor_scalar(ms.t[0:n, 0:ncols], ss.t[0:n, 0:ncols], EPS, None, op0=ALU.add),
                  reads=[ss], writes=[ms])
        else:
            cx.op(POOL, lambda: nc.gpsimd.tensor_scalar(ms.t[0:n, 0:ncols], ss.t[0:n, 0:ncols], scale, EPS,
                                                        op0=ALU.mult, op1=ALU.add),
                  reads=[ss], writes=[ms])
        cx.op(POOL, lambda: nc.gpsimd.tensor_tensor(rs.t[0:n, 0:ncols], ms.t[0:n, 0:ncols], neghalf.t[0:n, 0:ncols],
                                                    op=ALU.pow),
              reads=[ms, neghalf], writes=[rs])
        return rs

    def mm_group(out_ap, pairs, reads, bank):
        n = len(pairs)
        cx.group(PE, [lambda i=i, l=l, r=r: nc.tensor.matmul(out_ap, l, r, start=(i == 0), stop=(i == n - 1))
                      for i, (l, r) in enumerate(pairs)], reads=reads, writes=[bank])

    tmp_es = contextlib.ExitStack()
    tmp_tbs = []

    def sbtmp(name, shape, dt):
        tb = TB(tmp_es.enter_context(nc.sbuf_tensor(name, list(shape), dt)))
        tmp_tbs.append(tb)
        return tb

    ones_f = sbtmp("ones_f", [128, 128], F32)
    tmp_f = sbtmp("tmp_f", [128, 128], F32)
    cx.op(POOL, lambda: nc.gpsimd.memset(neghalf.t[:], -0.5), writes=[neghalf])
    cx.op(POOL, lambda: nc.gpsimd.memset(ones_f.t[:], 1.0), writes=[ones_f])
    cx.op(POOL, lambda: nc.gpsimd.affine_select(tmp_f.t[:], ones_f.t[:], pattern=[[-1, 128]], compare_op=ALU.is_equal,
                                                fill=0.0, base=0, channel_multiplier=1),
          reads=[ones_f], writes=[tmp_f])
    cx.op(DVE, lambda: nc.vector.tensor_copy(ident.t[:], tmp_f.t[:]), reads=[tmp_f], writes=[ident])
    cx.op(POOL, lambda: nc.gpsimd.affine_select(tmp_f.t[:], ones_f.t[:], pattern=[[1, 128]], compare_op=ALU.is_equal,
                                                fill=0.0, base=-127, channel_multiplier=1),
          reads=[ones_f, tmp_f], writes=[tmp_f])
    cx.op(DVE, lambda: nc.vector.tensor_copy(jmat.t[:], tmp_f.t[:]), reads=[tmp_f], writes=[jmat])
    tril_f = sbtmp("tril_f", [128, 128], F32)
    cx.op(POOL, lambda: nc.gpsimd.affine_select(tril_f.t[:], ones_f.t[:], pattern=[[-1, 128]], compare_op=ALU.is_ge,
                                                fill=0.0, base=0, channel_multiplier=1),
          reads=[ones_f], writes=[tril_f])
    cx.op(POOL, lambda: nc.gpsimd.memset(vaug.t[:, :, :, 64:65], 1.0), writes=[va_half[0], va_half[1]])

    def convert(plist):
      for p in plist:
        for (src, r0, nk, c0, ncol), (off, _, _) in zip(p["parts"], p["views"]):
            step = 4 if nk >= 8 else nk
            for k0 in range(0, nk, step):
                kk = min(step, nk - k0)
                cx.dma(POOL, p["dram"].ap()[:, off + k0 * ncol: off + (k0 + kk) * ncol].rearrange("p (k c) -> p k c", k=kk),
                       src.ap()[r0 + k0 * 128: r0 + (k0 + kk) * 128, c0:c0 + ncol].rearrange("(k p) c -> p k c", p=128),
                       writes=[p["trk"]], sem=p["sem"])


    convert(pieces[0:4])


    def finish():
        cx.wait(SP, out_trk.w)
        for E in (PE, ACT, DVE, POOL):
            cx.wait(SP, {E.sem: E.cnt})
        es.close()
        return nc
    if DBG <= 1:
        return finish()
    early = {}
    for s_ in range(2):
        xs_ = xstage_rot.next()
        cx.dma(SP, xs_.t[:], xp.ap()[0, s_ * 128:(s_ + 1) * 128, :], writes=[xs_])
        early[s_] = xs_
    cx.dma(SP, gcol_attn.t[:], g_attn.ap().rearrange("(c p) -> p c", p=128), writes=[gcol_attn], allow_slow_non_contiguous=True)
    cx.dma(SP, gv_bc.t[:], g_v.ap().partition_broadcast(128), writes=[gv_bc])

    def late_param_loads():
        cx.dma(SP, bs_t.t[:], b_sp.ap().rearrange("g t -> t g"), writes=[bs_t], allow_slow_non_contiguous=True)
        cx.dma(SP, gcol_out.t[:, 0:4], g_oa.ap().rearrange("(c p) -> p c", p=128), writes=[gcol_out],
               allow_slow_non_contiguous=True)
        cx.dma(SP, gcol_out.t[:, 4:8], g_ob.ap().rearrange("(c p) -> p c", p=128), writes=[gcol_out],
               allow_slow_non_contiguous=True)
        for (dst, src) in [(gcol_ffn, g_ffn), (gcol_ple, g_ple)]:
            cx.dma(SP, dst.t[:], src.ap().rearrange("(c p) -> p c", p=128), writes=[dst], allow_slow_non_contiguous=True)
        cx.dma(SP, gfin_bc.t[:], g_fin.ap().partition_broadcast(128), writes=[gfin_bc])

    if DBG <= 2:
        return finish()
    wsf = sbtmp("wsf", [128, 8, 128], F32)
    wsb = sbtmp("wsb", [128, 8, 128], BF16)
    cx.dma(SP, wsf.t[:], w_sp.ap().rearrange("g t s -> t g s"), writes=[wsf])
    cx.op(DVE, lambda: nc.vector.tensor_tensor(wsb.t[:], wsf.t[:], tril_f.t[:].unsqueeze(1).to_broadcast([128, 8, 128]),
                                               op=ALU.mult), reads=[wsf, tril_f], writes=[wsb])
    cx.group(PE, [lambda g=g: nc.tensor.transpose(tp.t[:, g * 128:(g + 1) * 128], wsb.t[:, g, :], ident.t[:])
                  for g in range(8)], reads=[wsb, ident], writes=[tp])
    cx.op(DVE, lambda: nc.vector.tensor_copy(wsT.t[:], tp.t[:].rearrange("p (g t) -> p g t", g=8)),
          reads=[tp], writes=[wsT])

    if DBG <= 3:
        return finish()
    e_sb = sbtmp("e_sb", [8, 512], F32)
    ext_d = nc.dram_tensor("ext_d", [8, 512], F32)
    ext_trk = Trk()
    cx.dma(SP, e_sb.t[:, 64:193], relb.ap(), writes=[e_sb])
    cx.op(DVE, lambda: nc.vector.tensor_copy(e_sb.t[:, 0:64], e_sb.t[:, 64:65].to_broadcast([8, 64])),
          reads=[e_sb], writes=[e_sb])
    cx.op(DVE, lambda: nc.vector.tensor_copy(e_sb.t[:, 193:512], e_sb.t[:, 192:193].to_broadcast([8, 319])),
          reads=[e_sb], writes=[e_sb])
    cx.dma(SP, ext_d.ap(), e_sb.t[:], reads=[e_sb], writes=[ext_trk])
    hk_f = sbtmp("hk_f", [128, 8, 128], F32)
    hk_b = sbtmp("hk_b", [128, 8, 128], BF16)
    cvec = sbtmp("cvec", [128, 8], F32)
    cneg = sbtmp("cneg", [128, 8], F32)
    cx.dma(SP, cvec.t[:], bass.AP(relb, 128, [[0, 128], [129, 8]]), writes=[cvec], allow_slow_non_contiguous=True)
    cx.op(DVE, lambda: nc.vector.tensor_scalar(cneg.t[:], cvec.t[:], -1.0, None, op0=ALU.mult), reads=[cvec], writes=[cneg])
    for coff, e0 in [(0, 1), (128, 129)]:
        cx.dma(SP, hk_f.t[:], bass.AP(ext_d, e0, [[1, 128], [512, 8], [1, 128]]), reads=[ext_trk], writes=[hk_f])
        cx.op(DVE, lambda: nc.vector.tensor_copy(hk_b.t[:], hk_f.t[:]), reads=[hk_f], writes=[hk_b])
        for hq in range(2):
            bk = banks.next()
            mm_group(bk.t[:], [(jmat.t[:], hk_b.t[:, hq * 4:(hq + 1) * 4, :])], [jmat, hk_b], bk)
            for hh in range(4):
                h = hq * 4 + hh
                cx.op(ACT, act_fn(E43.t[:, h, coff:coff + 128], bk.t[:, hh * 128:(hh + 1) * 128], AF.Exp,
                                  bias=cneg.t[:, h:h + 1]), reads=[bk, cneg], writes=[E43])
            bk.release()
    cx.op(DVE, lambda: nc.vector.memset(E43.t[64:128, :, 0:64], 0.0), reads=[E43], writes=[E43])
    cx.op(DVE, lambda: nc.vector.memset(E0.t[:], 1.0), writes=[E0])
    cx.op(DVE, lambda: nc.vector.memset(E0.t[0:64, :, 64:128], 0.0), reads=[E0], writes=[E0])

    late_param_loads()
    if DBG <= 4:
        return finish()
    tmp_es.close()
    big = nc.alloc_sbuf_tensor("big", [128, 14336], BF16)
    big32 = big.bitcast(F32)
    guT = TB(big.ap()[:, 0:NJ * T].rearrange("p (j t) -> p j t", j=NJ))
    PT_rot = Rot([TB(big.ap()[:, i * 3072:(i + 1) * 3072].rearrange("p (k h q) -> p k h q", k=6, h=2)) for i in range(2)])
    for pt_ in PT_rot.items:
        pt_.k = [Trk() for _ in range(6)]
    qT = TB(big.ap()[:, 6144:10240].rearrange("p (h t) -> p h t", h=8))
    qT4 = big.ap()[:, 6144:10240].rearrange("p (c e t) -> p c e t", c=4, e=2)
    ya_t = [TB(big32.ap()[:, 5120 + i * 512:5120 + (i + 1) * 512]) for i in range(4)]
    alias_small = [k_ for pt_ in PT_rot.items for k_ in pt_.k] + [qT] + ya_t
    for tb in tmp_tbs:
        for dst in [guT] + alias_small:
            _merge(dst.r, tb.r)
            _merge(dst.r, tb.w)
            _merge(dst.r, tb.prev)

    def handoff(srcs, dsts):
        for d_ in dsts:
            for s_ in srcs:
                _merge(d_.r, s_.r)
                _merge(d_.r, s_.w)
                _merge(d_.r, s_.prev)

    if DBG <= 5:
        convert(pieces[4:])
        for p in pieces:
            cx.wait(SP, p['trk'].w)
        return finish()
    total_tiles = NSEQ * (SEQ // T) + 1
    tile_piece_order = ["uv", "vb", "qk", "wo"] + ["gu%d" % i for i in range(len(JG))] + ["dn0", "dn1", "pg"]
    fetch_seq = tile_piece_order * total_tiles
    FS = {"next": 0, "q": [], "taken": []}

    def _inuse():
        return len(FS["taken"])

    def fetch_one():
        i = FS["next"]
        if i >= len(fetch_seq):
            return False
        assert len(FS["q"]) + _inuse() < 3
        p = piece_by_name[fetch_seq[i]]
        slot = wslots.next()
        cx.dma(SP, slot.t[:, 0:p["n"]], p["dram"].ap(), reads=[p["trk"]], writes=[slot])
        FS["q"].append((fetch_seq[i], slot, p))
        FS["next"] = i + 1
        return True

    def refill():
        while len(FS["q"]) + _inuse() < 3:
            if not fetch_one():
                break

    def take(name):
        if not FS["q"]:
            assert fetch_one()
        nm, slot, p = FS["q"].pop(0)
        assert nm == name, (nm, name)
        FS["taken"].append([name, slot, False])
        views = [slot.t[:, off:off + nk * ncol].rearrange("p (k c) -> p k c", k=nk) for (off, nk, ncol) in p["views"]]
        return slot, views

    def release_w(name):
        for e in FS["taken"]:
            if e[0] == name and not e[2]:
                e[2] = True
                break
        else:
            raise AssertionError(name)
        while FS["taken"] and FS["taken"][0][2]:
            e = FS["taken"].pop(0)
            e[1].release()
        refill()

    actT_s2 = [[Trk() for _ in range(4)] for _ in range(2)]

    class TC:
        def __init__(self, xin, pin, yout, ntok, b0, k_out=None, v_out=None, va_out=None, pre=None):
            self.xin, self.pin, self.yout, self.ntok, self.b0 = xin, pin, yout, ntok, b0
            self.k_out, self.v_out, self.va_out, self.pre = k_out, v_out, va_out, pre
            self.nsub = (ntok + 127) // 128
            self.rows = [min(128, ntok - 128 * s) for s in range(self.nsub)]
            self.half = (b0 % 8) // 4
            self.ring0 = (b0 % 8) * 128
            self.w = {}
            self.d = {}
            self.ab = 0
            self.first = False

    def get_w(tc, name):
        if name not in tc.w:
            tc.w[name] = take(name)
        return tc.w[name]

    def norm_front(segs, n):
        xn = xn_rot.next()
        ss = stat_rot.next()
        off = 0
        for i, (stb, sap, width) in enumerate(segs):
            cx.op(ACT, act_fn(xn.t[0:n, off:off + width], sap, AF.Square, scale=1.0 / math.sqrt(width),
                              accum_out=ss.t[0:n, i:i + 1]), reads=[stb], writes=[ss, xn])
            off += width
        rs = pool_rstd(ss, n, ncols=len(segs))
        off = 0
        for i, (stb, sap, width) in enumerate(segs):
            cx.op(DVE, lambda off=off, sap=sap, width=width, i=i: nc.vector.tensor_scalar(
                xn.t[0:n, off:off + width], sap, rs.t[0:n, i:i + 1], None, op0=ALU.mult),
                reads=[stb, rs], writes=[xn])
            off += width
        return xn, off // 128

    def norm_back(xn, nch, n, gcols, dst_trks, dst_ap):
        tp = tp_rot.next()
        cx.group(PE, [lambda c=c: nc.tensor.transpose(tp.t[:, c * 128:c * 128 + n], xn.t[0:n, c * 128:(c + 1) * 128],
                                                      ident.t[0:n, 0:n]) for c in range(nch)],
                 reads=[xn, ident], writes=[tp])
        xn.release()
        src = tp.t[:, 0:nch * 128].rearrange("p (c t) -> p c t", c=nch)[:, :, 0:n]
        cx.op(DVE, lambda: nc.vector.tensor_tensor(dst_ap, src, gcols.t[:, 0:nch].unsqueeze(2).to_broadcast([128, nch, n]),
                                                   op=ALU.mult),
              reads=[tp, gcols], writes=dst_trks)
        tp.release()

    def s_xl(tc, s):
        n = tc.rows[s]
        if tc.first and s in early:
            tc.d[("xs", s)] = early.pop(s)
            return
        xs_ = xstage_rot.next()
        cx.dma(SP, xs_.t[0:n, :], tc.xin[s * 128:s * 128 + n, :], writes=[xs_])
        tc.d[("xs", s)] = xs_

    def s_xr(tc, s):
        n = tc.rows[s]
        cx.dma(SP, h_t[s].t[0:n, :], tc.xin[s * 128:s * 128 + n, :], writes=[h_t[s]])

    def s_pl(tc, s):
        n = tc.rows[s]
        pt_ = p_rot.next()
        cx.dma(SP, pt_.t[0:n, :], tc.pin[s * 128:s * 128 + n, :], writes=[pt_])
        pb = pb_rot.next()
        cx.op(DVE, lambda: nc.vector.tensor_copy(pb.t[0:n, :], pt_.t[0:n, :]), reads=[pt_], writes=[pb])
        pt_.release()
        tc.d[("p", s)] = pb

    def s_Af(tc, s):
        n = tc.rows[s]
        xs_ = tc.d.pop(("xs", s))
        tc.d[("A", s)] = norm_front([(xs_, xs_.t[0:n, :], D)], n)
        xs_.release()

    def s_Ab(tc, s):
        actT = actT2[tc.ab]
        actT_s = actT_s2[tc.ab]
        n = tc.rows[s]
        xn, nch = tc.d.pop(("A", s))
        norm_back(xn, nch, n, gcol_attn, [actT_s[s]], actT.t[:, 0:nch, s * 128:s * 128 + n])

    def s_uvb(tc, s):
        actT = actT2[tc.ab]
        actT_s = actT_s2[tc.ab]
        n = tc.rows[s]
        wuv, (vuv,) = get_w(tc, "uv")
        wvb, (vvb,) = get_w(tc, "vb")
        slot = (tc.b0 + s) % 8
        lhs = [actT.t[:, kc, s * 128:s * 128 + n] for kc in range(8)]
        bu = banks.next()
        mm_group(bu.t[0:n, :], [(lhs[kc], vuv[:, kc, 0:512]) for kc in range(8)], [wuv, actT_s[s]], bu)
        gu = f512.next()
        cx.op(ACT, act_fn(gu.t[0:n, :], bu.t[0:n, :], AF.Gelu_apprx_tanh), reads=[bu], writes=[gu])
        bu.release()
        bv = banks.next()
        mm_group(bv.t[0:n, :], [(lhs[kc], vuv[:, kc, 512:1024]) for kc in range(8)], [wuv, actT_s[s]], bv)
        gv = f512.next()
        cx.op(ACT, act_fn(gv.t[0:n, :], bv.t[0:n, :], AF.Gelu_apprx_tanh), reads=[bv], writes=[gv])
        bv.release()
        bb = banks.next()
        mm_group(bb.t[0:n, :], [(lhs[kc], vvb[:, kc, 0:512]) for kc in range(8)], [wvb, actT_s[s]], bb)
        if tc.v_out is not None:
            vo = f512.next()
            cx.op(ACT, lambda: nc.scalar.copy(vo.t[0:n, :], bb.t[0:n, :]), reads=[bb], writes=[vo])
            cx.op(DVE, lambda: nc.vector.tensor_copy(vaug.t[0:n, slot, :, 0:64], vo.t[0:n, :].rearrange("p (h d) -> p h d", h=8)),
                  reads=[vo], writes=[va_half[slot // 4]])
            cx.dma(SP, tc.v_out[s * 128:s * 128 + n, :], vo.t[0:n, :], reads=[vo], writes=[out_trk])
            vo.release()
        else:
            cx.op(DVE, lambda: nc.vector.tensor_copy(vaug.t[0:n, slot, :, 0:64], bb.t[0:n, :].rearrange("p (h d) -> p h d", h=8)),
                  reads=[bb], writes=[va_half[slot // 4]])
        bb.release()
        g2 = f512.next()
        cx.op(ACT, act_fn(g2.t[0:n, :], gv.t[0:n, :], AF.Square), reads=[gv], writes=[g2])
        ssv = stat_rot.next()
        cx.op(DVE, lambda: nc.vector.tensor_reduce(ssv.t[0:n, 0:8], g2.t[0:n, :].rearrange("p (g d) -> p g d", g=8),
                                                   axis=AX.X, op=ALU.add), reads=[g2], writes=[ssv])
        rsv = pool_rstd(ssv, n, ncols=8, scale=1.0 / 64.0)
        cx.op(DVE, lambda: nc.vector.tensor_tensor(
            g2.t[0:n, :].rearrange("p (g d) -> p g d", g=8), gv.t[0:n, :].rearrange("p (g d) -> p g d", g=8),
            rsv.t[0:n, 0:8].unsqueeze(2).to_broadcast([n, 8, 64]), op=ALU.mult), reads=[gv, rsv, g2], writes=[g2])
        gv.release()
        vn = vn_rot.next()
        cx.op(DVE, lambda: nc.vector.tensor_tensor(vn.t[0:n, :], g2.t[0:n, :], gv_bc.t[0:n, :], op=ALU.mult),
              reads=[g2, gv_bc], writes=[vn])
        if tc.va_out is not None:
            vf = f512.next()
            cx.op(DVE, lambda: nc.vector.tensor_tensor(vf.t[0:n, :], g2.t[0:n, :], gv_bc.t[0:n, :], op=ALU.mult),
                  reads=[g2, gv_bc], writes=[vf])
            cx.dma(SP, tc.va_out[s * 128:s * 128 + n, :], vf.t[0:n, :], reads=[vf], writes=[out_trk])
            vf.release()
        g2.release()
        tc.d[("gu", s)] = gu
        tc.d[("vn", s)] = vn
        if s == tc.nsub - 1:
            tc.w.pop("uv"); tc.w.pop("vb")
            release_w("uv"); release_w("vb")

    def s_C(tc, s):
        n = tc.rows[s]
        gu = tc.d.pop(("gu", s))
        vn = tc.d.pop(("vn", s))
        bm = banks.next()
        cx.group(PE, [lambda g=g: nc.tensor.matmul(bm.t[0:n, g * 64:(g + 1) * 64], wsT.t[0:n, g, 0:n],
                                                   vn.t[0:n, g * 64:(g + 1) * 64], start=True, stop=True)
                      for g in range(8)], reads=[wsT, vn], writes=[bm])
        vn.release()
        mt = f512.next()
        cx.op(DVE, lambda: nc.vector.tensor_tensor(
            mt.t[0:n, :].rearrange("p (g d) -> p g d", g=8), bm.t[0:n, :].rearrange("p (g d) -> p g d", g=8),
            bs_t.t[0:n, :].unsqueeze(2).to_broadcast([n, 8, 64]), op=ALU.add), reads=[bm, bs_t], writes=[mt])
        bm.release()
        cx.op(DVE, lambda: nc.vector.tensor_tensor(ya_t[s].t[0:n, :], mt.t[0:n, :], gu.t[0:n, :], op=ALU.mult),
              reads=[mt, gu], writes=[ya_t[s]])
        mt.release()
        gu.release()

    def s_B1(tc):
        actT = actT2[tc.ab]
        actT_s = actT_s2[tc.ab]
        ntok = tc.ntok
        wqk, (vqk,) = get_w(tc, "qk")
        rd = [wqk] + [actT_s[s] for s in range(tc.nsub)]
        for oc in range(8):
            bk = banks.next()
            mm_group(bk.t[:, 0:ntok], [(vqk[:, kc, oc * 128:(oc + 1) * 128], actT.t[:, kc, 0:ntok]) for kc in range(8)], rd, bk)
            if oc < 4:
                if oc == 0:
                    cx.op(POOL, lambda: nc.gpsimd.memset(qT4[64:128, :, 0, 0:ntok], 0.0), writes=[qT])
                    cx.op(POOL, lambda: nc.gpsimd.memset(qT4[0:64, :, 1, 0:ntok], 0.0), writes=[qT])
                cx.op(ACT, act_fn(qT.t[0:64, 2 * oc, 0:ntok], bk.t[0:64, 0:ntok], AF.Copy, scale=0.125), reads=[bk], writes=[qT])
                cx.op(ACT, act_fn(qT.t[64:128, 2 * oc + 1, 0:ntok], bk.t[64:128, 0:ntok], AF.Copy, scale=0.125), reads=[bk], writes=[qT])
            else:
                cx.op(DVE, lambda bk=bk, oc=oc: nc.vector.tensor_copy(kT.t[:, oc - 4, tc.ring0:tc.ring0 + ntok], bk.t[:, 0:ntok]),
                      reads=[bk], writes=[kT_half[tc.half]])
            bk.release()
        if tc.k_out is not None:
            for s in range(tc.nsub):
                n = tc.rows[s]
                bk = banks.next()
                mm_group(bk.t[0:n, :], [(actT.t[:, kc, s * 128:s * 128 + n], vqk[:, kc, 512:1024]) for kc in range(8)],
                         [wqk, actT_s[s]], bk)
                ko = f512.next()
                cx.op(ACT, lambda bk=bk, ko=ko, n=n: nc.scalar.copy(ko.t[0:n, :], bk.t[0:n, :]), reads=[bk], writes=[ko])
                cx.dma(SP, tc.k_out[s * 128:s * 128 + n, :], ko.t[0:n, :], reads=[ko], writes=[out_trk])
                bk.release()
                ko.release()
        tc.w.pop("qk")
        release_w("qk")

    def s_D2(tc, pi, extra=()):
        ss = [s_ for s_ in (2 * pi, 2 * pi + 1) if s_ < tc.nsub]
        nq = [tc.rows[s_] for s_ in ss]
        qb0 = tc.b0 + 2 * pi
        ybs = [f512.next() for _ in ss]
        two = len(ss) == 2

        def vis(kbl, qh):
            return (kbl <= 4) if qh == 0 else (kbl >= 1)

        kbls = [kbl for kbl in range(6) if qb0 - 4 + kbl >= 0 and any(vis(kbl, qh) for qh in range(len(ss)))]

        def scores(h2):
            PT = PT_rot.next()
            for kbl in kbls:
                kb = qb0 - 4 + kbl
                qhs = [qh for qh in range(len(ss)) if vis(kbl, qh)]
                c0 = qhs[0] * 128
                c1 = qhs[-1] * 128 + nq[qhs[-1]]
                kslot = (kb % 8) * 128
                bk = banks.next()
                fns = [lambda hh=hh: nc.tensor.matmul(
                    bk.t[:, hh * 256 + c0:hh * 256 + c1], kT.t[:, h2, kslot:kslot + 128],
                    qT.t[:, 2 * h2 + hh, 2 * pi * 128 + c0:2 * pi * 128 + c1], start=True, stop=True) for hh in range(2)]
                cx.group(PE, fns, reads=[qT, kT_half[(kb % 8) // 4]], writes=[bk])
                cx.op(ACT, act_fn(PT.t[:, kbl, :, c0:c1], bk.t[:, :].rearrange("p (h q) -> p h q", h=2)[:, :, c0:c1], AF.Exp),
                      reads=[bk], writes=[PT.k[kbl]])
                bk.release()
                hp = slice(2 * h2, 2 * h2 + 2)
                mul = None
                if kbl == 0 and 0 in qhs:
                    mul = (E0, E0.t[:, hp, 0:nq[0]], 0, nq[0])
                elif kbl == 1 and 1 in qhs:
                    mul = (E0, E0.t[:, hp, 0:nq[1]], 128, 128 + nq[1])
                elif kbl == 3 and 0 in qhs:
                    mul = (E43, E43.t[:, hp, 128:128 + nq[0]], 0, nq[0])
                elif kbl == 4:
                    if two:
                        mul = (E43, E43.t[:, hp, 0:128 + nq[1]], 0, 128 + nq[1])
                    else:
                        mul = (E43, E43.t[:, hp, 0:nq[0]], 0, nq[0])
                elif kbl == 5 and 1 in qhs:
                    mul = (E43, E43.t[:, hp, 0:nq[1]], 128, 128 + nq[1])
                if mul is not None:
                    etb, eap, m0, m1 = mul
                    cx.op(DVE, lambda kbl=kbl, eap=eap, m0=m0, m1=m1: nc.vector.tensor_tensor(
                        PT.t[:, kbl, :, m0:m1], PT.t[:, kbl, :, m0:m1], eap, op=ALU.mult),
                        reads=[PT.k[kbl], etb], writes=[PT.k[kbl]])
            return PT

        def pv(h2, PT):
            bo = banks.next()
            fns = []
            for qh in range(len(ss)):
                for hh in range(2):
                    h = 2 * h2 + hh
                    kk = [kbl for kbl in kbls if vis(kbl, qh)]
                    for i, kbl in enumerate(kk):
                        kb = qb0 - 4 + kbl
                        fns.append(lambda qh=qh, hh=hh, h=h, kbl=kbl, kb=kb, i=i, last=len(kk) - 1: nc.tensor.matmul(
                            bo.t[0:nq[qh], (qh * 2 + hh) * 65:(qh * 2 + hh + 1) * 65],
                            PT.t[0:128, kbl, hh, qh * 128:qh * 128 + nq[qh]], vaug.t[0:128, kb % 8, h, :],
                            start=(i == 0), stop=(i == last)))
            cx.group(PE, fns, reads=[PT.k[kbl] for kbl in kbls] + [va_half[0], va_half[1]], writes=[bo])
            PT.release()
            for qh in range(len(ss)):
                n = nq[qh]
                rc = stat_rot.next()
                bov = bo.t[0:n, qh * 130:(qh + 1) * 130].rearrange("p (h d) -> p h d", h=2)
                cx.op(DVE, lambda rc=rc, bov=bov, n=n: nc.vector.reciprocal(rc.t[0:n, 0:2].unsqueeze(2), bov[:, :, 64:65]),
                      reads=[bo], writes=[rc])
                cx.op(DVE, lambda rc=rc, bov=bov, n=n, qh=qh: nc.vector.tensor_tensor(
                    ybs[qh].t[0:n, h2 * 128:(h2 + 1) * 128].rearrange("p (h d) -> p h d", h=2), bov[:, :, 0:64],
                    rc.t[0:n, 0:2].unsqueeze(2).to_broadcast([n, 2, 64]), op=ALU.mult), reads=[bo, rc], writes=[ybs[qh]])
            bo.release()

        pts = {0: scores(0)}
        for h2 in range(4):
            if h2 + 1 < 4:
                pts[h2 + 1] = scores(h2 + 1)
            pv(h2, pts.pop(h2))
            if h2 < len(extra):
                for fn in extra[h2]:
                    fn()
        for qh, s_ in enumerate(ss):
            tc.d[("yb", s_)] = ybs[qh]

    def s_Ef(tc, s):
        n = tc.rows[s]
        yb = tc.d.pop(("yb", s))
        tc.d[("E", s)] = norm_front([(ya_t[s], ya_t[s].t[0:n, :], DA), (yb, yb.t[0:n, :], DA)], n)
        yb.release()

    def s_Eb1(tc, s):
        n = tc.rows[s]
        xn, nch = tc.d.pop(("E", s))
        ynT = ynT_rot.next()
        norm_back(xn, nch, n, gcol_out, [ynT], ynT.t[:, 0:nch, 0:n])
        tc.d[("ynT", s)] = ynT

    def s_Eb2(tc, s):
        n = tc.rows[s]
        ynT = tc.d.pop(("ynT", s))
        wwo, (vwo,) = get_w(tc, "wo")
        for hf in range(2):
            bk = banks.next()
            mm_group(bk.t[0:n, :], [(ynT.t[:, kc, 0:n], vwo[:, kc, hf * 512:(hf + 1) * 512]) for kc in range(8)], [wwo, ynT], bk)
            cx.op(DVE, lambda bk=bk, hf=hf: nc.vector.tensor_tensor(
                h_t[s].t[0:n, hf * 512:(hf + 1) * 512], h_t[s].t[0:n, hf * 512:(hf + 1) * 512], bk.t[0:n, :], op=ALU.add),
                reads=[bk, h_t[s]], writes=[h_t[s]])
            bk.release()
        ynT.release()
        if s == tc.nsub - 1:
            tc.w.pop("wo")
            release_w("wo")

    def s_Ff(tc, s):
        n = tc.rows[s]
        tc.d[("F", s)] = norm_front([(h_t[s], h_t[s].t[0:n, :], D)], n)

    def s_Fb(tc, s):
        actT = actT2[tc.ab]
        actT_s = actT_s2[tc.ab]
        n = tc.rows[s]
        xn, nch = tc.d.pop(("F", s))
        norm_back(xn, nch, n, gcol_ffn, [actT_s[s]], actT.t[:, 0:nch, s * 128:s * 128 + n])

    def s_Pp(tc, s):
        n = tc.rows[s]
        pb = tc.d.pop(("p", s))
        tp = tp_rot.next()
        cx.group(PE, [lambda c=c: nc.tensor.transpose(tp.t[:, c * 128:c * 128 + n], pb.t[0:n, c * 128:(c + 1) * 128],
                                                      ident.t[0:n, 0:n]) for c in range(2)], reads=[pb, ident], writes=[tp])
        pb.release()
        cx.op(ACT, lambda: nc.scalar.copy(pT.t[:, :, s * 128:s * 128 + n],
                                          tp.t[:, 0:256].rearrange("p (c t) -> p c t", c=2)[:, :, 0:n]), reads=[tp], writes=[pT])
        tp.release()

    def g_part(tc, gi, jj, c0, c1):
        actT = actT2[tc.ab]
        actT_s = actT_s2[tc.ab]
        if not tc.d.get("g_handoff"):
            handoff(alias_small, [guT])
            tc.d["g_handoff"] = True
        wgu, (vg, vu) = get_w(tc, "gu%d" % gi)
        j = JG[gi][0] + jj
        rd = [wgu] + [actT_s[s_] for s_ in range(c0 // 128, (c1 + 127) // 128)]
        bg = banks.next()
        mm_group(bg.t[:, c0:c1], [(vg[:, kc, jj * 128:(jj + 1) * 128], actT.t[:, kc, c0:c1]) for kc in range(8)], rd, bg)
        bu = banks.next()
        mm_group(bu.t[:, c0:c1], [(vu[:, kc, jj * 128:(jj + 1) * 128], actT.t[:, kc, c0:c1]) for kc in range(8)], rd, bu)
        sl = f512.next()
        cx.op(ACT, act_fn(sl.t[:, c0:c1], bg.t[:, c0:c1], AF.Silu), reads=[bg], writes=[sl])
        cx.op(DVE, lambda: nc.vector.tensor_tensor(guT.t[:, j, c0:c1], sl.t[:, c0:c1], bu.t[:, c0:c1], op=ALU.mult),
              reads=[sl, bu], writes=[guT])
        bg.release(); bu.release(); sl.release()

    def s_G(tc, nxt_steps, nxt, split):
        ntok = tc.ntok
        for gi, (j0, nj) in enumerate(JG):
            for jj in range(nj):
                g_part(tc, gi, jj, 256 if (split and gi in (0, 1)) else 0, ntok)
            tc.w.pop("gu%d" % gi)
            release_w("gu%d" % gi)
            if nxt_steps:
                run(nxt_steps.pop(0), nxt)
        while nxt_steps:
            run(nxt_steps.pop(0), nxt)
        if nxt is not None and nxt.pre is not None:
            nxt.pre()
        tc.d.pop("g_handoff", None)

    def s_H(tc, hf, s):
        n = tc.rows[s]
        wdn, (vdn,) = get_w(tc, "dn%d" % hf)
        bk = banks.next()
        mm_group(bk.t[0:n, :], [(guT.t[:, j, s * 128:s * 128 + n], vdn[:, j, :]) for j in range(NJ)], [wdn, guT], bk)
        cx.op(DVE, lambda: nc.vector.tensor_tensor(
            h_t[s].t[0:n, hf * 512:(hf + 1) * 512], h_t[s].t[0:n, hf * 512:(hf + 1) * 512], bk.t[0:n, :], op=ALU.add),
            reads=[bk, h_t[s]], writes=[h_t[s]])
        bk.release()
        if s == tc.nsub - 1:
            tc.w.pop("dn%d" % hf)
            release_w("dn%d" % hf)

    def s_If(tc, s):
        n = tc.rows[s]
        tc.d[("I", s)] = norm_front([(h_t[s], h_t[s].t[0:n, :], D)], n)

    def s_Ib(tc, s):
        actT = actT2[tc.ab]
        actT_s = actT_s2[tc.ab]
        n = tc.rows[s]
        xn, nch = tc.d.pop(("I", s))
        norm_back(xn, nch, n, gcol_ple, [actT_s[s]], actT.t[:, 0:nch, s * 128:s * 128 + n])

    def s_J(tc, s):
        actT = actT2[tc.ab]
        actT_s = actT_s2[tc.ab]
        n = tc.rows[s]
        wpg, (vpg, vpp) = get_w(tc, "pg")
        for hf in range(2):
            bg = banks.next()
            mm_group(bg.t[0:n, :], [(actT.t[:, kc, s * 128:s * 128 + n], vpg[:, kc, hf * 512:(hf + 1) * 512]) for kc in range(8)],
                     [wpg, actT_s[s]], bg)
            bp = banks.next()
            mm_group(bp.t[0:n, :], [(pT.t[:, kc, s * 128:s * 128 + n], vpp[:, kc, hf * 512:(hf + 1) * 512]) for kc in range(2)],
                     [wpg, pT], bp)
            sg = f512.next()
            cx.op(ACT, act_fn(sg.t[0:n, :], bg.t[0:n, :], AF.Sigmoid), reads=[bg], writes=[sg])
            gp = f512.next()
            cx.op(DVE, lambda sg=sg, bp=bp, gp=gp: nc.vector.tensor_tensor(gp.t[0:n, :], sg.t[0:n, :], bp.t[0:n, :], op=ALU.mult),
                  reads=[sg, bp], writes=[gp])
            cx.op(POOL, lambda gp=gp, hf=hf: nc.gpsimd.tensor_tensor(
                h_t[s].t[0:n, hf * 512:(hf + 1) * 512], h_t[s].t[0:n, hf * 512:(hf + 1) * 512], gp.t[0:n, :], op=ALU.add),
                reads=[gp, h_t[s]], writes=[h_t[s]])
            bg.release(); bp.release(); sg.release(); gp.release()
        if s == tc.nsub - 1:
            tc.w.pop("pg")
            release_w("pg")

    def s_K(tc, s):
        n = tc.rows[s]
        scr = xn_rot.next()
        ss = stat_rot.next()
        cx.op(ACT, act_fn(scr.t[0:n, :], h_t[s].t[0:n, :], AF.Square, scale=1.0 / 32.0, accum_out=ss.t[0:n, 0:1]),
              reads=[h_t[s]], writes=[ss, scr])
        scr.release()
        rs = pool_rstd(ss, n)
        cx.op(DVE, lambda: nc.vector.scalar_tensor_tensor(
            h_t[s].t[0:n, :], h_t[s].t[0:n, :], rs.t[0:n, 0:1], gfin_bc.t[0:n, :], op0=ALU.mult, op1=ALU.mult),
            reads=[h_t[s], rs, gfin_bc], writes=[h_t[s]])
        cx.dma(SP, tc.yout[s * 128:s * 128 + n, :], h_t[s].t[0:n, :], reads=[h_t[s]], writes=[out_trk])

    def a_steps(tc):
        ns = tc.nsub
        steps = []
        for k in range(ns + 4):
            st = []
            if k == 0:
                st += [(s_xl, j) for j in range(min(2, ns))]
            if k >= 1 and k - 1 < ns:
                st += [(s_Af, k - 1)]
                if k + 1 < ns:
                    st += [(s_xl, k + 1)]
            if k >= 3 and k - 3 < ns:
                st += [(s_Ab, k - 3)]
            if st:
                steps.append(st)
        return steps

    def head_steps(tc):
        ns = tc.nsub
        steps = []
        for k in range(ns):
            steps.append([(s_uvb, k)])
        return steps

    def run(st_list, tc):
        for fn, s in st_list:
            fn(tc, s)

    def emit_all(tiles):
        for i, tc in enumerate(tiles):
            tc.ab = i % 2
        t0 = tiles[0]
        t0.first = True
        ast = a_steps(t0)
        run(ast.pop(0), t0)
        for s in range(t0.nsub):
            s_xr(t0, s)
        for st in ast:
            run(st, t0)
        convert(pieces[4:])
        refill()
        prev = None
        for ti, tc in enumerate(tiles):
            ns = tc.nsub
            nxt = tiles[ti + 1] if ti + 1 < len(tiles) else None
            for s in range(ns):
                s_pl(tc, s)
            pend = list(range(prev.nsub)) if prev is not None else []
            for st in head_steps(tc):
                run(st, tc)
                if pend:
                    sp = pend.pop(0)
                    s_K(prev, sp)
                    if sp < ns:
                        s_xr(tc, sp)
            for sp in pend:
                s_K(prev, sp)
                if sp < ns:
                    s_xr(tc, sp)
            if DBG <= 7:
                raise StopBuild()
            s_B1(tc)
            for s_ in range(ns - 1):
                s_C(tc, s_)
            npair = (ns + 1) // 2
            for pi in range(npair):
                extra = []
                if pi >= 1:
                    a_, b_ = 2 * pi - 2, 2 * pi - 1
                    extra = [
                        [lambda a_=a_: s_Eb1(tc, a_), lambda a_=a_: s_Pp(tc, a_)],
                        [lambda a_=a_: s_Eb2(tc, a_), lambda a_=a_: s_Ff(tc, a_), lambda b_=b_: s_Eb1(tc, b_), lambda b_=b_: s_Pp(tc, b_)],
                        [lambda b_=b_: s_Eb2(tc, b_), lambda b_=b_: s_Ff(tc, b_), lambda a_=a_: s_Fb(tc, a_)],
                        [lambda b_=b_: s_Fb(tc, b_)],
                    ]
                s_D2(tc, pi, extra)
                if pi == 0:
                    s_C(tc, ns - 1)
                for s_ in (2 * pi, 2 * pi + 1):
                    if s_ < ns:
                        s_Ef(tc, s_)
            last = [s_ for s_ in (2 * npair - 2, 2 * npair - 1) if s_ < ns]
            split = ns == 4
            gq = [(lambda gi_=gi_, jj=jj: g_part(tc, gi_, jj, 0, 256)) for gi_ in (0, 1) for jj in range(JG[gi_][1])] if split else []

            def gpop(k):
                for _ in range(k):
                    if gq:
                        gq.pop(0)()

            gpop(3)
            for s_ in last:
                s_Eb1(tc, s_)
                s_Pp(tc, s_)
            gpop(2)
            for s_ in last:
                s_Eb2(tc, s_)
                s_Ff(tc, s_)
            gpop(3)
            for s_ in last:
                s_Fb(tc, s_)
            gpop(10)
            if DBG <= 9:
                raise StopBuild()
            s_G(tc, a_steps(nxt) if nxt is not None else [], nxt, split)
            for s in range(ns):
                s_H(tc, 0, s)
            for s in range(ns):
                s_H(tc, 1, s)
                s_If(tc, s)
                if s >= 1:
                    s_Ib(tc, s - 1)
            handoff([guT], alias_small)
            for s in range(ns):
                if s == min(2, ns - 1):
                    s_Ib(tc, ns - 1)
                s_J(tc, s)
            prev = tc
            if DBG <= 15 and ti + 1 >= int(env.get('KT', '1')):
                break
        for sp in range(prev.nsub):
            s_K(prev, sp)

    def sample_pre():
        for blk in range(4):
            cf = f512.next()
            cx.dma(SP, cf.t[:], ck.ap()[blk * 128:(blk + 1) * 128, :], writes=[cf])
            cb = vn_rot.next()
            cx.op(DVE, lambda cf=cf, cb=cb: nc.vector.tensor_copy(cb.t[:], cf.t[:]), reads=[cf], writes=[cb])
            tp = tp_rot.next()
            cx.group(PE, [lambda c=c, cb=cb, tp=tp: nc.tensor.transpose(tp.t[:, c * 128:(c + 1) * 128],
                                                                         cb.t[:, c * 128:(c + 1) * 128], ident.t[:])
                          for c in range(4)], reads=[cb, ident], writes=[tp])
            cf.release(); cb.release()
            cx.op(DVE, lambda blk=blk, tp=tp: nc.vector.tensor_copy(kT.t[:, :, blk * 128:(blk + 1) * 128],
                                                                    tp.t[:, 0:512].rearrange("p (c t) -> p c t", c=4)),
                  reads=[tp], writes=[kT_half[0]])
            tp.release()
            vf = f512.next()
            cx.dma(SP, vf.t[:], cv.ap()[blk * 128:(blk + 1) * 128, :], writes=[vf])
            cx.op(DVE, lambda vf=vf, blk=blk: nc.vector.tensor_copy(vaug.t[:, blk, :, 0:64],
                                                                    vf.t[:].rearrange("p (h d) -> p h d", h=8)),
                  reads=[vf], writes=[va_half[0]])
            vf.release()

    tiles = []
    for q in range(NSEQ):
        for ti in range(SEQ // T):
            last = ti == SEQ // T - 1
            tiles.append(TC(xp.ap()[q, ti * T:(ti + 1) * T, :], pp.ap()[q, ti * T:(ti + 1) * T, :],
                            yp.ap()[q, ti * T:(ti + 1) * T, :], T, ti * 4,
                            k_out=nkp.ap()[q] if last else None, v_out=nvp.ap()[q] if last else None))
    if DBG > 16:
        tiles.append(TC(xs.ap(), ps.ap(), ys.ap(), NS, 4, k_out=nks.ap(), v_out=nvs.ap(), va_out=nvas.ap(), pre=sample_pre))
    try:
        emit_all(tiles)
    except StopBuild:
        pass
    return finish()


_CACHE = {}


def kernel(x_prompt, x_sample, p_prompt, p_sample, cache_band_k, cache_band_v,
           g_attn, w_in, g_v, w_spatial, b_spatial, rel_bias, g_out_a, g_out_b, w_out,
           g_ffn, w_gate, w_up, w_down, g_ple, w_ple_gate, w_ple_proj, g_final):
    f = lambda a: np.ascontiguousarray(np.asarray(a, dtype=np.float32))
    if "nc" not in _CACHE:
        _CACHE["nc"] = build()
    nc = _CACHE["nc"]
    shared = {
        "g_attn": f(g_attn[0]), "w_in": f(w_in[0]), "g_v": f(g_v[0]).reshape(DA), "w_sp": f(w_spatial[0]),
        "b_sp": f(b_spatial[0]), "relb": f(rel_bias[0]), "g_oa": f(g_out_a[0]), "g_ob": f(g_out_b[0]),
        "w_out": f(w_out[0]), "g_ffn": f(g_ffn[0]), "w_gate": f(w_gate[0]), "w_up": f(w_up[0]),
        "w_down": f(w_down[0]), "g_ple": f(g_ple[0]), "w_pg": f(w_ple_gate[0]), "w_pp": f(w_ple_proj[0]),
        "g_fin": f(g_final),
    }
    x_prompt = np.asarray(x_prompt); p_prompt = np.asarray(p_prompt)
    in_maps = []
    for c in range(NCORES):
        m = dict(shared)
        m["xp"] = f(x_prompt[NSEQ * c:NSEQ * (c + 1)])
        m["pp"] = f(p_prompt[0, NSEQ * c:NSEQ * (c + 1)])
        m["xs"] = f(x_sample[c])
        m["ps"] = f(p_sample[0, c])
        m["ck"] = f(cache_band_k[0, c]).reshape(NCACHE, DA)
        m["cv"] = f(cache_band_v[0, c]).reshape(NCACHE, DA)
        in_maps.append(m)
    res = run_bass_kernel_spmd(nc, in_maps, core_ids=list(range(NCORES)))
    R = res.results
    y_prompt = np.concatenate([R[c]["yp"] for c in range(NCORES)], axis=0).reshape(16, SEQ, D)
    y_sample = np.stack([R[c]["ys"] for c in range(NCORES)], axis=0).reshape(8, NS, D)
    nk_p = np.concatenate([R[c]["nkp"] for c in range(NCORES)], axis=0).reshape(1, 16, T, 8, 64)
    nv_p = np.concatenate([R[c]["nvp"] for c in range(NCORES)], axis=0).reshape(1, 16, T, 8, 64)
    nk_s = np.stack([R[c]["nks"] for c in range(NCORES)], axis=0).reshape(1, 8, NS, 8, 64)
    nv_s = np.stack([R[c]["nvs"] for c in range(NCORES)], axis=0).reshape(1, 8, NS, 8, 64)
    nva_s = np.stack([R[c]["nvas"] for c in range(NCORES)], axis=0).reshape(1, 8, NS, 8, 64)
    return (y_prompt.astype(np.float32), y_sample.astype(np.float32), nk_p.astype(np.float32), nv_p.astype(np.float32),
            nk_s.astype(np.float32), nv_s.astype(np.float32), nva_s.astype(np.float32))
```

```python
import contextlib
import math

import numpy as np
import concourse.bass as bass
import concourse.mybir as mybir
from concourse.bass_utils import run_bass_kernel_spmd

F32 = mybir.dt.float32
BF16 = mybir.dt.bfloat16
AF = mybir.ActivationFunctionType
ALU = mybir.AluOpType
AX = mybir.AxisListType

D = 1024
DA = 512
DIN = 2560
DFF = 2816
NJ = DFF // 128
DPLE = 256
SEQ = 2048
NSEQ = 2
NS = 64
NCACHE = 512
T = 512
EPS = 1e-6
NCORES = 8
SLOT_ELEMS = 11264
NEG = -30000.0


class StopBuild(Exception):
    pass


class Trk:
    def __init__(self):
        self.w = {}
        self.r = {}
        self.prev = {}


def _merge(dst, src):
    for k, v in src.items():
        if dst.get(k, 0) < v:
            dst[k] = v


class TB(Trk):
    def __init__(self, t):
        super().__init__()
        self.t = t
        self.free = True

    def release(self):
        assert not self.free
        self.free = True


class Rot:
    def __init__(self, items, check=True):
        self.items = items
        self.i = 0
        self.check = check

    def next(self):
        n = len(self.items)
        if not self.check:
            it = self.items[self.i % n]
            self.i += 1
            return it
        for k in range(n):
            it = self.items[(self.i + k) % n]
            if it.free:
                it.free = False
                self.i = (self.i + k + 1) % n
                return it
        raise AssertionError("rotating pool exhausted: a buffer was not released")


class Eng:
    def __init__(self, name, eng, sem_idx):
        self.name = name
        self.e = eng
        self.sem = sem_idx
        self.cnt = 0
        self.known = {}


class Ctx:
    def __init__(self, nc, es):
        self.nc = nc
        self.es = es
        self.sems = []
        self.hist = {}
        self.pe = self._eng("pe", nc.tensor)
        self.act = self._eng("act", nc.scalar)
        self.dve = self._eng("dve", nc.vector)
        self.pool = self._eng("pool", nc.gpsimd)
        self.sp = self._eng("sp", nc.sync)
        self.dma_pools = {
            "sp": Rot([[self.new_sem("dsp%d" % i), 0] for i in range(16)], check=False),
            "pool": Rot([[self.new_sem("dpl%d" % i), 0] for i in range(8)], check=False),
        }

    def new_sem(self, name):
        h = self.es.enter_context(self.nc.semaphore(name))
        self.sems.append(h)
        return len(self.sems) - 1

    def _eng(self, name, eng):
        return Eng(name, eng, self.new_sem("s_" + name))

    def wait(self, E, deps, embed=False):
        need = sorted(((k, v) for k, v in deps.items() if E.known.get(k, 0) < v), key=lambda kv: kv[0])
        todo = []
        for k, v in need:
            if E.known.get(k, 0) >= v:
                continue
            todo.append((k, v))
            E.known[k] = v
            _merge(E.known, self.hist.get((k, v), {}))
        last = None
        if embed and todo:
            last = todo.pop()
        for k, v in todo:
            E.e.wait_ge(self.sems[k], v)
        return last

    def _deps(self, reads, writes):
        deps = {}
        for b in reads:
            _merge(deps, b.w)
        for b in writes:
            if b.r:
                b.prev = dict(b.r)
                _merge(b.prev, b.w)
                b.r = {}
                b.w = {}
            _merge(deps, b.prev)
        return deps

    def _post(self, tok, reads, writes):
        k, v = tok
        for b in reads:
            if b.r.get(k, 0) < v:
                b.r[k] = v
        for b in writes:
            if b.w.get(k, 0) < v:
                b.w[k] = v

    def op(self, E, fn, reads=(), writes=()):
        last = self.wait(E, self._deps(reads, writes), embed=True)
        ins = fn()
        if last is not None:
            ins._wait_ge(self.sems[last[0]], last[1])
        E.cnt += 1
        ins.then_inc(self.sems[E.sem], 1)
        self.hist[(E.sem, E.cnt)] = dict(E.known)
        self._post((E.sem, E.cnt), reads, writes)

    def group(self, E, fns, reads=(), writes=()):
        last = self.wait(E, self._deps(reads, writes), embed=True)
        ins = None
        for i, fn in enumerate(fns):
            ins = fn()
            if i == 0 and last is not None:
                ins._wait_ge(self.sems[last[0]], last[1])
        E.cnt += 1
        ins.then_inc(self.sems[E.sem], 1)
        self.hist[(E.sem, E.cnt)] = dict(E.known)
        self._post((E.sem, E.cnt), reads, writes)

    def dma(self, Q, out, in_, reads=(), writes=(), sem=None, **kw):
        deps = self._deps(reads, writes)
        if sem is None:
            sl = self.dma_pools[Q.name].next()
            if sl[1] > 0:
                _merge(deps, {sl[0]: sl[1]})
        else:
            sl = sem
        self.wait(Q, deps)
        ins = Q.e.dma_start(out=out, in_=in_, **kw)
        sl[1] += 16
        ins.then_inc(self.sems[sl[0]], 16)
        self.hist[(sl[0], sl[1])] = dict(Q.known)
        self._post((sl[0], sl[1]), reads, writes)


def build():
    env = {}
    DBG = int(env.get('KDBG', '99'))
    KS = int(env.get('KS', '99'))
    nc = bass.Bass("TRN2", target_bir_lowering=False)
    es = contextlib.ExitStack()
    cx = Ctx(nc, es)
    PE, ACT, DVE, POOL, SP = cx.pe, cx.act, cx.dve, cx.pool, cx.sp

    def din(name, shape):
        return nc.dram_tensor(name, list(shape), F32, kind="ExternalInput")

    def dout(name, shape):
        return nc.dram_tensor(name, list(shape), F32, kind="ExternalOutput")

    xp = din("xp", [NSEQ, SEQ, D])
    xs = din("xs", [NS, D])
    pp = din("pp", [NSEQ, SEQ, DPLE])
    ps = din("ps", [NS, DPLE])
    ck = din("ck", [NCACHE, DA])
    cv = din("cv", [NCACHE, DA])
    g_attn = din("g_attn", [D])
    w_in = din("w_in", [D, DIN])
    g_v = din("g_v", [DA])
    w_sp = din("w_sp", [8, 128, 128])
    b_sp = din("b_sp", [8, 128])
    relb = din("relb", [8, 129])
    g_oa = din("g_oa", [DA])
    g_ob = din("g_ob", [DA])
    w_out = din("w_out", [D, D])
    g_ffn = din("g_ffn", [D])
    w_gate = din("w_gate", [D, DFF])
    w_up = din("w_up", [D, DFF])
    w_down = din("w_down", [DFF, D])
    g_ple = din("g_ple", [D])
    w_pg = din("w_pg", [D, D])
    w_pp = din("w_pp", [DPLE, D])
    g_fin = din("g_fin", [D])

    yp = dout("yp", [NSEQ, SEQ, D])
    ys = dout("ys", [NS, D])
    nkp = dout("nkp", [NSEQ, T, DA])
    nvp = dout("nvp", [NSEQ, T, DA])
    nks = dout("nks", [NS, DA])
    nvs = dout("nvs", [NS, DA])
    nvas = dout("nvas", [NS, DA])
    out_trk = Trk()

    def sb(name, shape, dt):
        return TB(nc.alloc_sbuf_tensor(name, list(shape), dt))

    def ps_(name, shape, dt):
        return TB(nc.alloc_psum_tensor(name, list(shape), dt))

    pieces = []

    def add_piece(name, parts):
        off = 0
        views = []
        for (src, r0, nk, c0, ncol) in parts:
            views.append((off, nk, ncol))
            off += nk * ncol
        assert off <= SLOT_ELEMS, (name, off)
        dr = nc.dram_tensor("wsc_" + name, [128, off], BF16)
        pieces.append(dict(name=name, parts=parts, views=views, n=off, dram=dr, trk=Trk(),
                           sem=[cx.new_sem("cv_" + name), 0]))

    add_piece("uv", [(w_in, 0, 8, 0, 1024)])
    add_piece("vb", [(w_in, 0, 8, 2048, 512)])
    add_piece("qk", [(w_in, 0, 8, 1024, 1024)])
    add_piece("wo", [(w_out, 0, 8, 0, 1024)])
    JG = [(0, 5), (5, 5), (10, 5), (15, 5), (20, 2)]
    for gi, (j0, nj) in enumerate(JG):
        add_piece("gu%d" % gi, [(w_gate, 0, 8, j0 * 128, nj * 128), (w_up, 0, 8, j0 * 128, nj * 128)])
    add_piece("dn0", [(w_down, 0, NJ, 0, 512)])
    add_piece("dn1", [(w_down, 0, NJ, 512, 512)])
    add_piece("pg", [(w_pg, 0, 8, 0, 1024), (w_pp, 0, 2, 0, 1024)])
    piece_by_name = {p["name"]: p for p in pieces}
    tile_order = [p["name"] for p in pieces]

    ident = sb("ident", [128, 128], BF16)
    jmat = sb("jmat", [128, 128], BF16)
    gcol_attn = sb("gc_attn", [128, 8], F32)
    gcol_ffn = sb("gc_ffn", [128, 8], F32)
    gcol_ple = sb("gc_ple", [128, 8], F32)
    gcol_out = sb("gc_out", [128, 8], F32)
    gfin_bc = sb("gfin_bc", [128, D], F32)
    gv_bc = sb("gv_bc", [128, DA], F32)
    bs_t = sb("bs_t", [128, 8], F32)
    wsT = sb("wsT", [128, 8, 128], BF16)
    E43 = sb("E43", [128, 8, 256], BF16)
    E0 = sb("E0", [128, 8, 128], BF16)
    neghalf = sb("neghalf", [128, 8], F32)

    h_t = [sb("h%d" % s, [128, D], F32) for s in range(4)]
    kT = sb("kT", [128, 4, 1024], BF16)
    kT_half = [Trk(), Trk()]
    vaug = sb("vaug", [128, 8, 8, 65], BF16)
    va_half = [Trk(), Trk()]
    actT2 = [sb("actT%d" % i, [128, 8, T], BF16) for i in range(2)]
    xstage_rot = Rot([sb("xst%d" % i, [128, D], F32) for i in range(2)])
    pT = sb("pT", [128, 2, T], BF16)

    xn_rot = Rot([sb("xn%d" % i, [128, D], BF16) for i in range(5)])
    f512 = Rot([sb("f512_%d" % i, [128, 512], F32) for i in range(7)])
    vn_rot = Rot([sb("vn%d" % i, [128, DA], BF16) for i in range(4)])
    ynT_rot = Rot([sb("ynT%d" % i, [128, 8, 128], BF16) for i in range(2)])
    p_rot = Rot([sb("p%d" % i, [128, DPLE], F32) for i in range(4)])
    pb_rot = Rot([sb("pb%d" % i, [128, DPLE], BF16) for i in range(4)])
    stat_rot = Rot([sb("st%d" % i, [128, 8], F32) for i in range(24)], check=False)
    wslots = Rot([sb("wslot%d" % i, [128, SLOT_ELEMS], BF16) for i in range(3)])

    tp_rot = Rot([ps_("tp%d" % i, [128, 1024], BF16) for i in range(3)])
    tp = tp_rot.items[0]
    banks = Rot([ps_("bank%d" % i, [128, 512], F32) for i in range(5)])

    def act_fn(out, in_, func, **kw):
        return lambda: nc.scalar.activation(out, in_, func, **kw)

    def rstd_from(ss_ap, n, ncols, scale=None):
        ms = stat_rot.next()
        rs = stat_rot.next()
        return ms, rs

    def pool_rstd(ss, n, ncols=1, scale=None):
        ms = stat_rot.next()
        rs = stat_rot.next()
        if scale is None:
            cx.op(POOL, lambda: nc.gpsimd.tensor_scalar(ms.t[0:n, 0:ncols], ss.t[0:n, 0:ncols], EPS, None, op0=ALU.add),
                  reads=[ss], writes=[ms])
        else:
            cx.op(POOL, lambda: nc.gpsimd.tensor_scalar(ms.t[0:n, 0:ncols], ss.t[0:n, 0:ncols], scale, EPS,
                                                        op0=ALU.mult, op1=ALU.add),
                  reads=[ss], writes=[ms])
        cx.op(POOL, lambda: nc.gpsimd.tensor_tensor(rs.t[0:n, 0:ncols], ms.t[0:n, 0:ncols], neghalf.t[0:n, 0:ncols],
                                                    op=ALU.pow),
              reads=[ms, neghalf], writes=[rs])
        return rs

    def mm_group(out_ap, pairs, reads, bank):
        n = len(pairs)
        cx.group(PE, [lambda i=i, l=l, r=r: nc.tensor.matmul(out_ap, l, r, start=(i == 0), stop=(i == n - 1))
                      for i, (l, r) in enumerate(pairs)], reads=reads, writes=[bank])

    tmp_es = contextlib.ExitStack()
    tmp_tbs = []

    def sbtmp(name, shape, dt):
        tb = TB(tmp_es.enter_context(nc.sbuf_tensor(name, list(shape), dt)))
        tmp_tbs.append(tb)
        return tb

    ones_f = sbtmp("ones_f", [128, 128], F32)
    tmp_f = sbtmp("tmp_f", [128, 128], F32)
    cx.op(POOL, lambda: nc.gpsimd.memset(neghalf.t[:], -0.5), writes=[neghalf])
    cx.op(POOL, lambda: nc.gpsimd.memset(ones_f.t[:], 1.0), writes=[ones_f])
    cx.op(POOL, lambda: nc.gpsimd.affine_select(tmp_f.t[:], ones_f.t[:], pattern=[[-1, 128]], compare_op=ALU.is_equal,
                                                fill=0.0, base=0, channel_multiplier=1),
          reads=[ones_f], writes=[tmp_f])
    cx.op(DVE, lambda: nc.vector.tensor_copy(ident.t[:], tmp_f.t[:]), reads=[tmp_f], writes=[ident])
    cx.op(POOL, lambda: nc.gpsimd.affine_select(tmp_f.t[:], ones_f.t[:], pattern=[[1, 128]], compare_op=ALU.is_equal,
                                                fill=0.0, base=-127, channel_multiplier=1),
          reads=[ones_f, tmp_f], writes=[tmp_f])
    cx.op(DVE, lambda: nc.vector.tensor_copy(jmat.t[:], tmp_f.t[:]), reads=[tmp_f], writes=[jmat])
    tril_f = sbtmp("tril_f", [128, 128], F32)
    cx.op(POOL, lambda: nc.gpsimd.affine_select(tril_f.t[:], ones_f.t[:], pattern=[[-1, 128]], compare_op=ALU.is_ge,
                                                fill=0.0, base=0, channel_multiplier=1),
          reads=[ones_f], writes=[tril_f])
    cx.op(POOL, lambda: nc.gpsimd.memset(vaug.t[:, :, :, 64:65], 1.0), writes=[va_half[0], va_half[1]])

    def convert(plist):
      for p in plist:
        for (src, r0, nk, c0, ncol), (off, _, _) in zip(p["parts"], p["views"]):
            step = 4 if nk >= 8 else nk
            for k0 in range(0, nk, step):
                kk = min(step, nk - k0)
                cx.dma(POOL, p["dram"].ap()[:, off + k0 * ncol: off + (k0 + kk) * ncol].rearrange("p (k c) -> p k c", k=kk),
                       src.ap()[r0 + k0 * 128: r0 + (k0 + kk) * 128, c0:c0 + ncol].rearrange("(k p) c -> p k c", p=128),
                       writes=[p["trk"]], sem=p["sem"])


    convert(pieces[0:4])


    def finish():
        cx.wait(SP, out_trk.w)
        for E in (PE, ACT, DVE, POOL):
            cx.wait(SP, {E.sem: E.cnt})
        es.close()
        return nc
    if DBG <= 1:
        return finish()
    early = {}
    for s_ in range(2):
        xs_ = xstage_rot.next()
        cx.dma(SP, xs_.t[:], xp.ap()[0, s_ * 128:(s_ + 1) * 128, :], writes=[xs_])
        early[s_] = xs_
    cx.dma(SP, gcol_attn.t[:], g_attn.ap().rearrange("(c p) -> p c", p=128), writes=[gcol_attn], allow_slow_non_contiguous=True)
    cx.dma(SP, gv_bc.t[:], g_v.ap().partition_broadcast(128), writes=[gv_bc])

    def late_param_loads():
        cx.dma(SP, bs_t.t[:], b_sp.ap().rearrange("g t -> t g"), writes=[bs_t], allow_slow_non_contiguous=True)
        cx.dma(SP, gcol_out.t[:, 0:4], g_oa.ap().rearrange("(c p) -> p c", p=128), writes=[gcol_out],
               allow_slow_non_contiguous=True)
        cx.dma(SP, gcol_out.t[:, 4:8], g_ob.ap().rearrange("(c p) -> p c", p=128), writes=[gcol_out],
               allow_slow_non_contiguous=True)
        for (dst, src) in [(gcol_ffn, g_ffn), (gcol_ple, g_ple)]:
            cx.dma(SP, dst.t[:], src.ap().rearrange("(c p) -> p c", p=128), writes=[dst], allow_slow_non_contiguous=True)
        cx.dma(SP, gfin_bc.t[:], g_fin.ap().partition_broadcast(128), writes=[gfin_bc])

    if DBG <= 2:
        return finish()
    wsf = sbtmp("wsf", [128, 8, 128], F32)
    wsb = sbtmp("wsb", [128, 8, 128], BF16)
    cx.dma(SP, wsf.t[:], w_sp.ap().rearrange("g t s -> t g s"), writes=[wsf])
    cx.op(DVE, lambda: nc.vector.tensor_tensor(wsb.t[:], wsf.t[:], tril_f.t[:].unsqueeze(1).to_broadcast([128, 8, 128]),
                                               op=ALU.mult), reads=[wsf, tril_f], writes=[wsb])
    cx.group(PE, [lambda g=g: nc.tensor.transpose(tp.t[:, g * 128:(g + 1) * 128], wsb.t[:, g, :], ident.t[:])
                  for g in range(8)], reads=[wsb, ident], writes=[tp])
    cx.op(DVE, lambda: nc.vector.tensor_copy(wsT.t[:], tp.t[:].rearrange("p (g t) -> p g t", g=8)),
          reads=[tp], writes=[wsT])

    if DBG <= 3:
        return finish()
    e_sb = sbtmp("e_sb", [8, 512], F32)
    ext_d = nc.dram_tensor("ext_d", [8, 512], F32)
    ext_trk = Trk()
    cx.dma(SP, e_sb.t[:, 64:193], relb.ap(), writes=[e_sb])
    cx.op(DVE, lambda: nc.vector.tensor_copy(e_sb.t[:, 0:64], e_sb.t[:, 64:65].to_broadcast([8, 64])),
          reads=[e_sb], writes=[e_sb])
    cx.op(DVE, lambda: nc.vector.tensor_copy(e_sb.t[:, 193:512], e_sb.t[:, 192:193].to_broadcast([8, 319])),
          reads=[e_sb], writes=[e_sb])
    cx.dma(SP, ext_d.ap(), e_sb.t[:], reads=[e_sb], writes=[ext_trk])
    hk_f = sbtmp("hk_f", [128, 8, 128], F32)
    hk_b = sbtmp("hk_b", [128, 8, 128], BF16)
    cvec = sbtmp("cvec", [128, 8], F32)
    cneg = sbtmp("cneg", [128, 8], F32)
    cx.dma(SP, cvec.t[:], bass.AP(relb, 128, [[0, 128], [129, 8]]), writes=[cvec], allow_slow_non_contiguous=True)
    cx.op(DVE, lambda: nc.vector.tensor_scalar(cneg.t[:], cvec.t[:], -1.0, None, op0=ALU.mult), reads=[cvec], writes=[cneg])
    for coff, e0 in [(0, 1), (128, 129)]:
        cx.dma(SP, hk_f.t[:], bass.AP(ext_d, e0, [[1, 128], [512, 8], [1, 128]]), reads=[ext_trk], writes=[hk_f])
        cx.op(DVE, lambda: nc.vector.tensor_copy(hk_b.t[:], hk_f.t[:]), reads=[hk_f], writes=[hk_b])
        for hq in range(2):
            bk = banks.next()
            mm_group(bk.t[:], [(jmat.t[:], hk_b.t[:, hq * 4:(hq + 1) * 4, :])], [jmat, hk_b], bk)
            for hh in range(4):
                h = hq * 4 + hh
                cx.op(ACT, act_fn(E43.t[:, h, coff:coff + 128], bk.t[:, hh * 128:(hh + 1) * 128], AF.Exp,
                                  bias=cneg.t[:, h:h + 1]), reads=[bk, cneg], writes=[E43])
            bk.release()
    cx.op(DVE, lambda: nc.vector.memset(E43.t[64:128, :, 0:64], 0.0), reads=[E43], writes=[E43])
    cx.op(DVE, lambda: nc.vector.memset(E0.t[:], 1.0), writes=[E0])
    cx.op(DVE, lambda: nc.vector.memset(E0.t[0:64, :, 64:128], 0.0), reads=[E0], writes=[E0])

    late_param_loads()
    if DBG <= 4:
        return finish()
    tmp_es.close()
    big = nc.alloc_sbuf_tensor("big", [128, 14336], BF16)
    big32 = big.bitcast(F32)
    guT = TB(big.ap()[:, 0:NJ * T].rearrange("p (j t) -> p j t", j=NJ))
    PT_rot = Rot([TB(big.ap()[:, i * 3072:(i + 1) * 3072].rearrange("p (k h q) -> p k h q", k=6, h=2)) for i in range(2)])
    for pt_ in PT_rot.items:
        pt_.k = [Trk() for _ in range(6)]
    qT = TB(big.ap()[:, 6144:10240].rearrange("p (h t) -> p h t", h=8))
    qT4 = big.ap()[:, 6144:10240].rearrange("p (c e t) -> p c e t", c=4, e=2)
    ya_t = [TB(big32.ap()[:, 5120 + i * 512:5120 + (i + 1) * 512]) for i in range(4)]
    alias_small = [k_ for pt_ in PT_rot.items for k_ in pt_.k] + [qT] + ya_t
    for tb in tmp_tbs:
        for dst in [guT] + alias_small:
            _merge(dst.r, tb.r)
            _merge(dst.r, tb.w)
            _merge(dst.r, tb.prev)

    def handoff(srcs, dsts):
        for d_ in dsts:
            for s_ in srcs:
                _merge(d_.r, s_.r)
                _merge(d_.r, s_.w)
                _merge(d_.r, s_.prev)

    if DBG <= 5:
        convert(pieces[4:])
        for p in pieces:
            cx.wait(SP, p['trk'].w)
        return finish()
    total_tiles = NSEQ * (SEQ // T) + 1
    tile_piece_order = ["uv", "vb", "qk", "wo"] + ["gu%d" % i for i in range(len(JG))] + ["dn0", "dn1", "pg"]
    fetch_seq = tile_piece_order * total_tiles
    FS = {"next": 0, "q": [], "taken": []}

    def _inuse():
        return len(FS["taken"])

    def fetch_one():
        i = FS["next"]
        if i >= len(fetch_seq):
            return False
        assert len(FS["q"]) + _inuse() < 3
        p = piece_by_name[fetch_seq[i]]
        slot = wslots.next()
        cx.dma(SP, slot.t[:, 0:p["n"]], p["dram"].ap(), reads=[p["trk"]], writes=[slot])
        FS["q"].append((fetch_seq[i], slot, p))
        FS["next"] = i + 1
        return True

    def refill():
        while len(FS["q"]) + _inuse() < 3:
            if not fetch_one():
                break

    def take(name):
        if not FS["q"]:
            assert fetch_one()
        nm, slot, p = FS["q"].pop(0)
        assert nm == name, (nm, name)
        FS["taken"].append([name, slot, False])
        views = [slot.t[:, off:off + nk * ncol].rearrange("p (k c) -> p k c", k=nk) for (off, nk, ncol) in p["views"]]
        return slot, views

    def release_w(name):
        for e in FS["taken"]:
            if e[0] == name and not e[2]:
                e[2] = True
                break
        else:
            raise AssertionError(name)
        while FS["taken"] and FS["taken"][0][2]:
            e = FS["taken"].pop(0)
            e[1].release()
        refill()

    actT_s2 = [[Trk() for _ in range(4)] for _ in range(2)]

    class TC:
        def __init__(self, xin, pin, yout, ntok, b0, k_out=None, v_out=None, va_out=None, pre=None):
            self.xin, self.pin, self.yout, self.ntok, self.b0 = xin, pin, yout, ntok, b0
            self.k_out, self.v_out, self.va_out, self.pre = k_out, v_out, va_out, pre
            self.nsub = (ntok + 127) // 128
            self.rows = [min(128, ntok - 128 * s) for s in range(self.nsub)]
            self.half = (b0 % 8) // 4
            self.ring0 = (b0 % 8) * 128
            self.w = {}
            self.d = {}
            self.ab = 0
            self.first = False

    def get_w(tc, name):
        if name not in tc.w:
            tc.w[name] = take(name)
        return tc.w[name]

    def norm_front(segs, n):
        xn = xn_rot.next()
        ss = stat_rot.next()
        off = 0
        for i, (stb, sap, width) in enumerate(segs):
            cx.op(ACT, act_fn(xn.t[0:n, off:off + width], sap, AF.Square, scale=1.0 / math.sqrt(width),
                              accum_out=ss.t[0:n, i:i + 1]), reads=[stb], writes=[ss, xn])
            off += width
        rs = pool_rstd(ss, n, ncols=len(segs))
        off = 0
        for i, (stb, sap, width) in enumerate(segs):
            cx.op(DVE, lambda off=off, sap=sap, width=width, i=i: nc.vector.tensor_scalar(
                xn.t[0:n, off:off + width], sap, rs.t[0:n, i:i + 1], None, op0=ALU.mult),
                reads=[stb, rs], writes=[xn])
            off += width
        return xn, off // 128

    def norm_back(xn, nch, n, gcols, dst_trks, dst_ap):
        tp = tp_rot.next()
        cx.group(PE, [lambda c=c: nc.tensor.transpose(tp.t[:, c * 128:c * 128 + n], xn.t[0:n, c * 128:(c + 1) * 128],
                                                      ident.t[0:n, 0:n]) for c in range(nch)],
                 reads=[xn, ident], writes=[tp])
        xn.release()
        src = tp.t[:, 0:nch * 128].rearrange("p (c t) -> p c t", c=nch)[:, :, 0:n]
        cx.op(DVE, lambda: nc.vector.tensor_tensor(dst_ap, src, gcols.t[:, 0:nch].unsqueeze(2).to_broadcast([128, nch, n]),
                                                   op=ALU.mult),
              reads=[tp, gcols], writes=dst_trks)
        tp.release()

    def s_xl(tc, s):
        n = tc.rows[s]
        if tc.first and s in early:
            tc.d[("xs", s)] = early.pop(s)
            return
        xs_ = xstage_rot.next()
        cx.dma(SP, xs_.t[0:n, :], tc.xin[s * 128:s * 128 + n, :], writes=[xs_])
        tc.d[("xs", s)] = xs_

    def s_xr(tc, s):
        n = tc.rows[s]
        cx.dma(SP, h_t[s].t[0:n, :], tc.xin[s * 128:s * 128 + n, :], writes=[h_t[s]])

    def s_pl(tc, s):
        n = tc.rows[s]
        pt_ = p_rot.next()
        cx.dma(SP, pt_.t[0:n, :], tc.pin[s * 128:s * 128 + n, :], writes=[pt_])
        pb = pb_rot.next()
        cx.op(DVE, lambda: nc.vector.tensor_copy(pb.t[0:n, :], pt_.t[0:n, :]), reads=[pt_], writes=[pb])
        pt_.release()
        tc.d[("p", s)] = pb

    def s_Af(tc, s):
        n = tc.rows[s]
        xs_ = tc.d.pop(("xs", s))
        tc.d[("A", s)] = norm_front([(xs_, xs_.t[0:n, :], D)], n)
        xs_.release()

    def s_Ab(tc, s):
        actT = actT2[tc.ab]
        actT_s = actT_s2[tc.ab]
        n = tc.rows[s]
        xn, nch = tc.d.pop(("A", s))
        norm_back(xn, nch, n, gcol_attn, [actT_s[s]], actT.t[:, 0:nch, s * 128:s * 128 + n])

    def s_uvb(tc, s):
        actT = actT2[tc.ab]
        actT_s = actT_s2[tc.ab]
        n = tc.rows[s]
        wuv, (vuv,) = get_w(tc, "uv")
        wvb, (vvb,) = get_w(tc, "vb")
        slot = (tc.b0 + s) % 8
        lhs = [actT.t[:, kc, s * 128:s * 128 + n] for kc in range(8)]
        bu = banks.next()
        mm_group(bu.t[0:n, :], [(lhs[kc], vuv[:, kc, 0:512]) for kc in range(8)], [wuv, actT_s[s]], bu)
        gu = f512.next()
        cx.op(ACT, act_fn(gu.t[0:n, :], bu.t[0:n, :], AF.Gelu_apprx_tanh), reads=[bu], writes=[gu])
        bu.release()
        bv = banks.next()
        mm_group(bv.t[0:n, :], [(lhs[kc], vuv[:, kc, 512:1024]) for kc in range(8)], [wuv, actT_s[s]], bv)
        gv = f512.next()
        cx.op(ACT, act_fn(gv.t[0:n, :], bv.t[0:n, :], AF.Gelu_apprx_tanh), reads=[bv], writes=[gv])
        bv.release()
        bb = banks.next()
        mm_group(bb.t[0:n, :], [(lhs[kc], vvb[:, kc, 0:512]) for kc in range(8)], [wvb, actT_s[s]], bb)
        if tc.v_out is not None:
            vo = f512.next()
            cx.op(ACT, lambda: nc.scalar.copy(vo.t[0:n, :], bb.t[0:n, :]), reads=[bb], writes=[vo])
            cx.op(DVE, lambda: nc.vector.tensor_copy(vaug.t[0:n, slot, :, 0:64], vo.t[0:n, :].rearrange("p (h d) -> p h d", h=8)),
                  reads=[vo], writes=[va_half[slot // 4]])
            cx.dma(SP, tc.v_out[s * 128:s * 128 + n, :], vo.t[0:n, :], reads=[vo], writes=[out_trk])
            vo.release()
        else:
            cx.op(DVE, lambda: nc.vector.tensor_copy(vaug.t[0:n, slot, :, 0:64], bb.t[0:n, :].rearrange("p (h d) -> p h d", h=8)),
                  reads=[bb], writes=[va_half[slot // 4]])
        bb.release()
        g2 = f512.next()
        cx.op(ACT, act_fn(g2.t[0:n, :], gv.t[0:n, :], AF.Square), reads=[gv], writes=[g2])
        ssv = stat_rot.next()
        cx.op(DVE, lambda: nc.vector.tensor_reduce(ssv.t[0:n, 0:8], g2.t[0:n, :].rearrange("p (g d) -> p g d", g=8),
                                                   axis=AX.X, op=ALU.add), reads=[g2], writes=[ssv])
        rsv = pool_rstd(ssv, n, ncols=8, scale=1.0 / 64.0)
        cx.op(DVE, lambda: nc.vector.tensor_tensor(
            g2.t[0:n, :].rearrange("p (g d) -> p g d", g=8), gv.t[0:n, :].rearrange("p (g d) -> p g d", g=8),
            rsv.t[0:n, 0:8].unsqueeze(2).to_broadcast([n, 8, 64]), op=ALU.mult), reads=[gv, rsv, g2], writes=[g2])
        gv.release()
        vn = vn_rot.next()
        cx.op(DVE, lambda: nc.vector.tensor_tensor(vn.t[0:n, :], g2.t[0:n, :], gv_bc.t[0:n, :], op=ALU.mult),
              reads=[g2, gv_bc], writes=[vn])
        if tc.va_out is not None:
            vf = f512.next()
            cx.op(DVE, lambda: nc.vector.tensor_tensor(vf.t[0:n, :], g2.t[0:n, :], gv_bc.t[0:n, :], op=ALU.mult),
                  reads=[g2, gv_bc], writes=[vf])
            cx.dma(SP, tc.va_out[s * 128:s * 128 + n, :], vf.t[0:n, :], reads=[vf], writes=[out_trk])
            vf.release()
        g2.release()
        tc.d[("gu", s)] = gu
        tc.d[("vn", s)] = vn
        if s == tc.nsub - 1:
            tc.w.pop("uv"); tc.w.pop("vb")
            release_w("uv"); release_w("vb")

    def s_C(tc, s):
        n = tc.rows[s]
        gu = tc.d.pop(("gu", s))
        vn = tc.d.pop(("vn", s))
        bm = banks.next()
        cx.group(PE, [lambda g=g: nc.tensor.matmul(bm.t[0:n, g * 64:(g + 1) * 64], wsT.t[0:n, g, 0:n],
                                                   vn.t[0:n, g * 64:(g + 1) * 64], start=True, stop=True)
                      for g in range(8)], reads=[wsT, vn], writes=[bm])
        vn.release()
        mt = f512.next()
        cx.op(DVE, lambda: nc.vector.tensor_tensor(
            mt.t[0:n, :].rearrange("p (g d) -> p g d", g=8), bm.t[0:n, :].rearrange("p (g d) -> p g d", g=8),
            bs_t.t[0:n, :].unsqueeze(2).to_broadcast([n, 8, 64]), op=ALU.add), reads=[bm, bs_t], writes=[mt])
        bm.release()
        cx.op(DVE, lambda: nc.vector.tensor_tensor(ya_t[s].t[0:n, :], mt.t[0:n, :], gu.t[0:n, :], op=ALU.mult),
              reads=[mt, gu], writes=[ya_t[s]])
        mt.release()
        gu.release()

    def s_B1(tc):
        actT = actT2[tc.ab]
        actT_s = actT_s2[tc.ab]
        ntok = tc.ntok
        wqk, (vqk,) = get_w(tc, "qk")
        rd = [wqk] + [actT_s[s] for s in range(tc.nsub)]
        for oc in range(8):
            bk = banks.next()
            mm_group(bk.t[:, 0:ntok], [(vqk[:, kc, oc * 128:(oc + 1) * 128], actT.t[:, kc, 0:ntok]) for kc in range(8)], rd, bk)
            if oc < 4:
                if oc == 0:
                    cx.op(POOL, lambda: nc.gpsimd.memset(qT4[64:128, :, 0, 0:ntok], 0.0), writes=[qT])
                    cx.op(POOL, lambda: nc.gpsimd.memset(qT4[0:64, :, 1, 0:ntok], 0.0), writes=[qT])
                cx.op(ACT, act_fn(qT.t[0:64, 2 * oc, 0:ntok], bk.t[0:64, 0:ntok], AF.Copy, scale=0.125), reads=[bk], writes=[qT])
                cx.op(ACT, act_fn(qT.t[64:128, 2 * oc + 1, 0:ntok], bk.t[64:128, 0:ntok], AF.Copy, scale=0.125), reads=[bk], writes=[qT])
            else:
                cx.op(DVE, lambda bk=bk, oc=oc: nc.vector.tensor_copy(kT.t[:, oc - 4, tc.ring0:tc.ring0 + ntok], bk.t[:, 0:ntok]),
                      reads=[bk], writes=[kT_half[tc.half]])
            bk.release()
        if tc.k_out is not None:
            for s in range(tc.nsub):
                n = tc.rows[s]
                bk = banks.next()
                mm_group(bk.t[0:n, :], [(actT.t[:, kc, s * 128:s * 128 + n], vqk[:, kc, 512:1024]) for kc in range(8)],
                         [wqk, actT_s[s]], bk)
                ko = f512.next()
                cx.op(ACT, lambda bk=bk, ko=ko, n=n: nc.scalar.copy(ko.t[0:n, :], bk.t[0:n, :]), reads=[bk], writes=[ko])
                cx.dma(SP, tc.k_out[s * 128:s * 128 + n, :], ko.t[0:n, :], reads=[ko], writes=[out_trk])
                bk.release()
                ko.release()
        tc.w.pop("qk")
        release_w("qk")

    def s_D2(tc, pi, extra=()):
        ss = [s_ for s_ in (2 * pi, 2 * pi + 1) if s_ < tc.nsub]
        nq = [tc.rows[s_] for s_ in ss]
        qb0 = tc.b0 + 2 * pi
        ybs = [f512.next() for _ in ss]
        two = len(ss) == 2

        def vis(kbl, qh):
            return (kbl <= 4) if qh == 0 else (kbl >= 1)

        kbls = [kbl for kbl in range(6) if qb0 - 4 + kbl >= 0 and any(vis(kbl, qh) for qh in range(len(ss)))]

        def scores(h2):
            PT = PT_rot.next()
            for kbl in kbls:
                kb = qb0 - 4 + kbl
                qhs = [qh for qh in range(len(ss)) if vis(kbl, qh)]
                c0 = qhs[0] * 128
                c1 = qhs[-1] * 128 + nq[qhs[-1]]
                kslot = (kb % 8) * 128
                bk = banks.next()
                fns = [lambda hh=hh: nc.tensor.matmul(
                    bk.t[:, hh * 256 + c0:hh * 256 + c1], kT.t[:, h2, kslot:kslot + 128],
                    qT.t[:, 2 * h2 + hh, 2 * pi * 128 + c0:2 * pi * 128 + c1], start=True, stop=True) for hh in range(2)]
                cx.group(PE, fns, reads=[qT, kT_half[(kb % 8) // 4]], writes=[bk])
                cx.op(ACT, act_fn(PT.t[:, kbl, :, c0:c1], bk.t[:, :].rearrange("p (h q) -> p h q", h=2)[:, :, c0:c1], AF.Exp),
                      reads=[bk], writes=[PT.k[kbl]])
                bk.release()
                hp = slice(2 * h2, 2 * h2 + 2)
                mul = None
                if kbl == 0 and 0 in qhs:
                    mul = (E0, E0.t[:, hp, 0:nq[0]], 0, nq[0])
                elif kbl == 1 and 1 in qhs:
                    mul = (E0, E0.t[:, hp, 0:nq[1]], 128, 128 + nq[1])
                elif kbl == 3 and 0 in qhs:
                    mul = (E43, E43.t[:, hp, 128:128 + nq[0]], 0, nq[0])
                elif kbl == 4:
                    if two:
                        mul = (E43, E43.t[:, hp, 0:128 + nq[1]], 0, 128 + nq[1])
                    else:
                        mul = (E43, E43.t[:, hp, 0:nq[0]], 0, nq[0])
                elif kbl == 5 and 1 in qhs:
                    mul = (E43, E43.t[:, hp, 0:nq[1]], 128, 128 + nq[1])
                if mul is not None:
                    etb, eap, m0, m1 = mul
                    cx.op(DVE, lambda kbl=kbl, eap=eap, m0=m0, m1=m1: nc.vector.tensor_tensor(
                        PT.t[:, kbl, :, m0:m1], PT.t[:, kbl, :, m0:m1], eap, op=ALU.mult),
                        reads=[PT.k[kbl], etb], writes=[PT.k[kbl]])
            return PT

        def pv(h2, PT):
            bo = banks.next()
            fns = []
            for qh in range(len(ss)):
                for hh in range(2):
                    h = 2 * h2 + hh
                    kk = [kbl for kbl in kbls if vis(kbl, qh)]
                    for i, kbl in enumerate(kk):
                        kb = qb0 - 4 + kbl
                        fns.append(lambda qh=qh, hh=hh, h=h, kbl=kbl, kb=kb, i=i, last=len(kk) - 1: nc.tensor.matmul(
                            bo.t[0:nq[qh], (qh * 2 + hh) * 65:(qh * 2 + hh + 1) * 65],
                            PT.t[0:128, kbl, hh, qh * 128:qh * 128 + nq[qh]], vaug.t[0:128, kb % 8, h, :],
                            start=(i == 0), stop=(i == last)))
            cx.group(PE, fns, reads=[PT.k[kbl] for kbl in kbls] + [va_half[0], va_half[1]], writes=[bo])
            PT.release()
            for qh in range(len(ss)):
                n = nq[qh]
                rc = stat_rot.next()
                bov = bo.t[0:n, qh * 130:(qh + 1) * 130].rearrange("p (h d) -> p h d", h=2)
                cx.op(DVE, lambda rc=rc, bov=bov, n=n: nc.vector.reciprocal(rc.t[0:n, 0:2].unsqueeze(2), bov[:, :, 64:65]),
                      reads=[bo], writes=[rc])
                cx.op(DVE, lambda rc=rc, bov=bov, n=n, qh=qh: nc.vector.tensor_tensor(
                    ybs[qh].t[0:n, h2 * 128:(h2 + 1) * 128].rearrange("p (h d) -> p h d", h=2), bov[:, :, 0:64],
                    rc.t[0:n, 0:2].unsqueeze(2).to_broadcast([n, 2, 64]), op=ALU.mult), reads=[bo, rc], writes=[ybs[qh]])
            bo.release()

        pts = {0: scores(0)}
        for h2 in range(4):
            if h2 + 1 < 4:
                pts[h2 + 1] = scores(h2 + 1)
            pv(h2, pts.pop(h2))
            if h2 < len(extra):
                for fn in extra[h2]:
                    fn()
        for qh, s_ in enumerate(ss):
            tc.d[("yb", s_)] = ybs[qh]

    def s_Ef(tc, s):
        n = tc.rows[s]
        yb = tc.d.pop(("yb", s))
        tc.d[("E", s)] = norm_front([(ya_t[s], ya_t[s].t[0:n, :], DA), (yb, yb.t[0:n, :], DA)], n)
        yb.release()

    def s_Eb1(tc, s):
        n = tc.rows[s]
        xn, nch = tc.d.pop(("E", s))
        ynT = ynT_rot.next()
        norm_back(xn, nch, n, gcol_out, [ynT], ynT.t[:, 0:nch, 0:n])
        tc.d[("ynT", s)] = ynT

    def s_Eb2(tc, s):
        n = tc.rows[s]
        ynT = tc.d.pop(("ynT", s))
        wwo, (vwo,) = get_w(tc, "wo")
        for hf in range(2):
            bk = banks.next()
            mm_group(bk.t[0:n, :], [(ynT.t[:, kc, 0:n], vwo[:, kc, hf * 512:(hf + 1) * 512]) for kc in range(8)], [wwo, ynT], bk)
            cx.op(DVE, lambda bk=bk, hf=hf: nc.vector.tensor_tensor(
                h_t[s].t[0:n, hf * 512:(hf + 1) * 512], h_t[s].t[0:n, hf * 512:(hf + 1) * 512], bk.t[0:n, :], op=ALU.add),
                reads=[bk, h_t[s]], writes=[h_t[s]])
            bk.release()
        ynT.release()
        if s == tc.nsub - 1:
            tc.w.pop("wo")
            release_w("wo")

    def s_Ff(tc, s):
        n = tc.rows[s]
        tc.d[("F", s)] = norm_front([(h_t[s], h_t[s].t[0:n, :], D)], n)

    def s_Fb(tc, s):
        actT = actT2[tc.ab]
        actT_s = actT_s2[tc.ab]
        n = tc.rows[s]
        xn, nch = tc.d.pop(("F", s))
        norm_back(xn, nch, n, gcol_ffn, [actT_s[s]], actT.t[:, 0:nch, s * 128:s * 128 + n])

    def s_Pp(tc, s):
        n = tc.rows[s]
        pb = tc.d.pop(("p", s))
        tp = tp_rot.next()
        cx.group(PE, [lambda c=c: nc.tensor.transpose(tp.t[:, c * 128:c * 128 + n], pb.t[0:n, c * 128:(c + 1) * 128],
                                                      ident.t[0:n, 0:n]) for c in range(2)], reads=[pb, ident], writes=[tp])
        pb.release()
        cx.op(ACT, lambda: nc.scalar.copy(pT.t[:, :, s * 128:s * 128 + n],
                                          tp.t[:, 0:256].rearrange("p (c t) -> p c t", c=2)[:, :, 0:n]), reads=[tp], writes=[pT])
        tp.release()

    def g_part(tc, gi, jj, c0, c1):
        actT = actT2[tc.ab]
        actT_s = actT_s2[tc.ab]
        if not tc.d.get("g_handoff"):
            handoff(alias_small, [guT])
            tc.d["g_handoff"] = True
        wgu, (vg, vu) = get_w(tc, "gu%d" % gi)
        j = JG[gi][0] + jj
        rd = [wgu] + [actT_s[s_] for s_ in range(c0 // 128, (c1 + 127) // 128)]
        bg = banks.next()
        mm_group(bg.t[:, c0:c1], [(vg[:, kc, jj * 128:(jj + 1) * 128], actT.t[:, kc, c0:c1]) for kc in range(8)], rd, bg)
        bu = banks.next()
        mm_group(bu.t[:, c0:c1], [(vu[:, kc, jj * 128:(jj + 1) * 128], actT.t[:, kc, c0:c1]) for kc in range(8)], rd, bu)
        sl = f512.next()
        cx.op(ACT, act_fn(sl.t[:, c0:c1], bg.t[:, c0:c1], AF.Silu), reads=[bg], writes=[sl])
        cx.op(DVE, lambda: nc.vector.tensor_tensor(guT.t[:, j, c0:c1], sl.t[:, c0:c1], bu.t[:, c0:c1], op=ALU.mult),
              reads=[sl, bu], writes=[guT])
        bg.release(); bu.release(); sl.release()

    def s_G(tc, nxt_steps, nxt, split):
        ntok = tc.ntok
        for gi, (j0, nj) in enumerate(JG):
            for jj in range(nj):
                g_part(tc, gi, jj, 256 if (split and gi in (0, 1)) else 0, ntok)
            tc.w.pop("gu%d" % gi)
            release_w("gu%d" % gi)
            if nxt_steps:
                run(nxt_steps.pop(0), nxt)
        while nxt_steps:
            run(nxt_steps.pop(0), nxt)
        if nxt is not None and nxt.pre is not None:
            nxt.pre()
        tc.d.pop("g_handoff", None)

    def s_H(tc, hf, s):
        n = tc.rows[s]
        wdn, (vdn,) = get_w(tc, "dn%d" % hf)
        bk = banks.next()
        mm_group(bk.t[0:n, :], [(guT.t[:, j, s * 128:s * 128 + n], vdn[:, j, :]) for j in range(NJ)], [wdn, guT], bk)
        cx.op(DVE, lambda: nc.vector.tensor_tensor(
            h_t[s].t[0:n, hf * 512:(hf + 1) * 512], h_t[s].t[0:n, hf * 512:(hf + 1) * 512], bk.t[0:n, :], op=ALU.add),
            reads=[bk, h_t[s]], writes=[h_t[s]])
        bk.release()
        if s == tc.nsub - 1:
            tc.w.pop("dn%d" % hf)
            release_w("dn%d" % hf)

    def s_If(tc, s):
        n = tc.rows[s]
        tc.d[("I", s)] = norm_front([(h_t[s], h_t[s].t[0:n, :], D)], n)

    def s_Ib(tc, s):
        actT = actT2[tc.ab]
        actT_s = actT_s2[tc.ab]
        n = tc.rows[s]
        xn, nch = tc.d.pop(("I", s))
        norm_back(xn, nch, n, gcol_ple, [actT_s[s]], actT.t[:, 0:nch, s * 128:s * 128 + n])

    def s_J(tc, s):
        actT = actT2[tc.ab]
        actT_s = actT_s2[tc.ab]
        n = tc.rows[s]
        wpg, (vpg, vpp) = get_w(tc, "pg")
        for hf in range(2):
            bg = banks.next()
            mm_group(bg.t[0:n, :], [(actT.t[:, kc, s * 128:s * 128 + n], vpg[:, kc, hf * 512:(hf + 1) * 512]) for kc in range(8)],
                     [wpg, actT_s[s]], bg)
            bp = banks.next()
            mm_group(bp.t[0:n, :], [(pT.t[:, kc, s * 128:s * 128 + n], vpp[:, kc, hf * 512:(hf + 1) * 512]) for kc in range(2)],
                     [wpg, pT], bp)
            sg = f512.next()
            cx.op(ACT, act_fn(sg.t[0:n, :], bg.t[0:n, :], AF.Sigmoid), reads=[bg], writes=[sg])
            gp = f512.next()
            cx.op(DVE, lambda sg=sg, bp=bp, gp=gp: nc.vector.tensor_tensor(gp.t[0:n, :], sg.t[0:n, :], bp.t[0:n, :], op=ALU.mult),
                  reads=[sg, bp], writes=[gp])
            cx.op(POOL, lambda gp=gp, hf=hf: nc.gpsimd.tensor_tensor(
                h_t[s].t[0:n, hf * 512:(hf + 1) * 512], h_t[s].t[0:n, hf * 512:(hf + 1) * 512], gp.t[0:n, :], op=ALU.add),
                reads=[gp, h_t[s]], writes=[h_t[s]])
            bg.release(); bp.release(); sg.release(); gp.release()
        if s == tc.nsub - 1:
            tc.w.pop("pg")
            release_w("pg")

    def s_K(tc, s):
        n = tc.rows[s]
        scr = xn_rot.next()
        ss = stat_rot.next()
        cx.op(ACT, act_fn(scr.t[0:n, :], h_t[s].t[0:n, :], AF.Square, scale=1.0 / 32.0, accum_out=ss.t[0:n, 0:1]),
              reads=[h_t[s]], writes=[ss, scr])
        scr.release()
        rs = pool_rstd(ss, n)
        cx.op(DVE, lambda: nc.vector.scalar_tensor_tensor(
            h_t[s].t[0:n, :], h_t[s].t[0:n, :], rs.t[0:n, 0:1], gfin_bc.t[0:n, :], op0=ALU.mult, op1=ALU.mult),
            reads=[h_t[s], rs, gfin_bc], writes=[h_t[s]])
        cx.dma(SP, tc.yout[s * 128:s * 128 + n, :], h_t[s].t[0:n, :], reads=[h_t[s]], writes=[out_trk])

    def a_steps(tc):
        ns = tc.nsub
        steps = []
        for k in range(ns + 4):
            st = []
            if k == 0:
                st += [(s_xl, j) for j in range(min(2, ns))]
            if k >= 1 and k - 1 < ns:
                st += [(s_Af, k - 1)]
                if k + 1 < ns:
                    st += [(s_xl, k + 1)]
            if k >= 3 and k - 3 < ns:
                st += [(s_Ab, k - 3)]
            if st:
                steps.append(st)
        return steps

    def head_steps(tc):
        ns = tc.nsub
        steps = []
        for k in range(ns):
            steps.append([(s_uvb, k)])
        return steps

    def run(st_list, tc):
        for fn, s in st_list:
            fn(tc, s)

    def emit_all(tiles):
        for i, tc in enumerate(tiles):
            tc.ab = i % 2
        t0 = tiles[0]
        t0.first = True
        ast = a_steps(t0)
        run(ast.pop(0), t0)
        for s in range(t0.nsub):
            s_xr(t0, s)
        for st in ast:
            run(st, t0)
        convert(pieces[4:])
        refill()
        prev = None
        for ti, tc in enumerate(tiles):
            ns = tc.nsub
            nxt = tiles[ti + 1] if ti + 1 < len(tiles) else None
            for s in range(ns):
                s_pl(tc, s)
            pend = list(range(prev.nsub)) if prev is not None else []
            for st in head_steps(tc):
                run(st, tc)
                if pend:
                    sp = pend.pop(0)
                    s_K(prev, sp)
                    if sp < ns:
                        s_xr(tc, sp)
            for sp in pend:
                s_K(prev, sp)
                if sp < ns:
                    s_xr(tc, sp)
            if DBG <= 7:
                raise StopBuild()
            s_B1(tc)
            for s_ in range(ns - 1):
                s_C(tc, s_)
            npair = (ns + 1) // 2
            for pi in range(npair):
                extra = []
                if pi >= 1:
                    a_, b_ = 2 * pi - 2, 2 * pi - 1
                    extra = [
                        [lambda a_=a_: s_Eb1(tc, a_), lambda a_=a_: s_Pp(tc, a_)],
                        [lambda a_=a_: s_Eb2(tc, a_), lambda a_=a_: s_Ff(tc, a_), lambda b_=b_: s_Eb1(tc, b_), lambda b_=b_: s_Pp(tc, b_)],
                        [lambda b_=b_: s_Eb2(tc, b_), lambda b_=b_: s_Ff(tc, b_), lambda a_=a_: s_Fb(tc, a_)],
                        [lambda b_=b_: s_Fb(tc, b_)],
                    ]
                s_D2(tc, pi, extra)
                if pi == 0:
                    s_C(tc, ns - 1)
                for s_ in (2 * pi, 2 * pi + 1):
                    if s_ < ns:
                        s_Ef(tc, s_)
            last = [s_ for s_ in (2 * npair - 2, 2 * npair - 1) if s_ < ns]
            split = ns == 4
            gq = [(lambda gi_=gi_, jj=jj: g_part(tc, gi_, jj, 0, 256)) for gi_ in (0, 1) for jj in range(JG[gi_][1])] if split else []

            def gpop(k):
                for _ in range(k):
                    if gq:
                        gq.pop(0)()

            gpop(3)
            for s_ in last:
                s_Eb1(tc, s_)
                s_Pp(tc, s_)
            gpop(2)
            for s_ in last:
                s_Eb2(tc, s_)
                s_Ff(tc, s_)
            gpop(3)
            for s_ in last:
                s_Fb(tc, s_)
            gpop(10)
            if DBG <= 9:
                raise StopBuild()
            s_G(tc, a_steps(nxt) if nxt is not None else [], nxt, split)
            for s in range(ns):
                s_H(tc, 0, s)
            for s in range(ns):
                s_H(tc, 1, s)
                s_If(tc, s)
                if s >= 1:
                    s_Ib(tc, s - 1)
            handoff([guT], alias_small)
            for s in range(ns):
                if s == min(2, ns - 1):
                    s_Ib(tc, ns - 1)
                s_J(tc, s)
            prev = tc
            if DBG <= 15 and ti + 1 >= int(env.get('KT', '1')):
                break
        for sp in range(prev.nsub):
            s_K(prev, sp)

    def sample_pre():
        for blk in range(4):
            cf = f512.next()
            cx.dma(SP, cf.t[:], ck.ap()[blk * 128:(blk + 1) * 128, :], writes=[cf])
            cb = vn_rot.next()
            cx.op(DVE, lambda cf=cf, cb=cb: nc.vector.tensor_copy(cb.t[:], cf.t[:]), reads=[cf], writes=[cb])
            tp = tp_rot.next()
            cx.group(PE, [lambda c=c, cb=cb, tp=tp: nc.tensor.transpose(tp.t[:, c * 128:(c + 1) * 128],
                                                                         cb.t[:, c * 128:(c + 1) * 128], ident.t[:])
                          for c in range(4)], reads=[cb, ident], writes=[tp])
            cf.release(); cb.release()
            cx.op(DVE, lambda blk=blk, tp=tp: nc.vector.tensor_copy(kT.t[:, :, blk * 128:(blk + 1) * 128],
                                                                    tp.t[:, 0:512].rearrange("p (c t) -> p c t", c=4)),
                  reads=[tp], writes=[kT_half[0]])
            tp.release()
            vf = f512.next()
            cx.dma(SP, vf.t[:], cv.ap()[blk * 128:(blk + 1) * 128, :], writes=[vf])
            cx.op(DVE, lambda vf=vf, blk=blk: nc.vector.tensor_copy(vaug.t[:, blk, :, 0:64],
                                                                    vf.t[:].rearrange("p (h d) -> p h d", h=8)),
                  reads=[vf], writes=[va_half[0]])
            vf.release()

    tiles = []
    for q in range(NSEQ):
        for ti in range(SEQ // T):
            last = ti == SEQ // T - 1
            tiles.append(TC(xp.ap()[q, ti * T:(ti + 1) * T, :], pp.ap()[q, ti * T:(ti + 1) * T, :],
                            yp.ap()[q, ti * T:(ti + 1) * T, :], T, ti * 4,
                            k_out=nkp.ap()[q] if last else None, v_out=nvp.ap()[q] if last else None))
    if DBG > 16:
        tiles.append(TC(xs.ap(), ps.ap(), ys.ap(), NS, 4, k_out=nks.ap(), v_out=nvs.ap(), va_out=nvas.ap(), pre=sample_pre))
    try:
        emit_all(tiles)
    except StopBuild:
        pass
    return finish()


_CACHE = {}


def kernel(x_prompt, x_sample, p_prompt, p_sample, cache_band_k, cache_band_v,
           g_attn, w_in, g_v, w_spatial, b_spatial, rel_bias, g_out_a, g_out_b, w_out,
           g_ffn, w_gate, w_up, w_down, g_ple, w_ple_gate, w_ple_proj, g_final):
    f = lambda a: np.ascontiguousarray(np.asarray(a, dtype=np.float32))
    if "nc" not in _CACHE:
        _CACHE["nc"] = build()
    nc = _CACHE["nc"]
    shared = {
        "g_attn": f(g_attn[0]), "w_in": f(w_in[0]), "g_v": f(g_v[0]).reshape(DA), "w_sp": f(w_spatial[0]),
        "b_sp": f(b_spatial[0]), "relb": f(rel_bias[0]), "g_oa": f(g_out_a[0]), "g_ob": f(g_out_b[0]),
        "w_out": f(w_out[0]), "g_ffn": f(g_ffn[0]), "w_gate": f(w_gate[0]), "w_up": f(w_up[0]),
        "w_down": f(w_down[0]), "g_ple": f(g_ple[0]), "w_pg": f(w_ple_gate[0]), "w_pp": f(w_ple_proj[0]),
        "g_fin": f(g_final),
    }
    x_prompt = np.asarray(x_prompt); p_prompt = np.asarray(p_prompt)
    in_maps = []
    for c in range(NCORES):
        m = dict(shared)
        m["xp"] = f(x_prompt[NSEQ * c:NSEQ * (c + 1)])
        m["pp"] = f(p_prompt[0, NSEQ * c:NSEQ * (c + 1)])
        m["xs"] = f(x_sample[c])
        m["ps"] = f(p_sample[0, c])
        m["ck"] = f(cache_band_k[0, c]).reshape(NCACHE, DA)
        m["cv"] = f(cache_band_v[0, c]).reshape(NCACHE, DA)
        in_maps.append(m)
    res = run_bass_kernel_spmd(nc, in_maps, core_ids=list(range(NCORES)))
    R = res.results
    y_prompt = np.concatenate([R[c]["yp"] for c in range(NCORES)], axis=0).reshape(16, SEQ, D)
    y_sample = np.stack([R[c]["ys"] for c in range(NCORES)], axis=0).reshape(8, NS, D)
    nk_p = np.concatenate([R[c]["nkp"] for c in range(NCORES)], axis=0).reshape(1, 16, T, 8, 64)
    nv_p = np.concatenate([R[c]["nvp"] for c in range(NCORES)], axis=0).reshape(1, 16, T, 8, 64)
    nk_s = np.stack([R[c]["nks"] for c in range(NCORES)], axis=0).reshape(1, 8, NS, 8, 64)
    nv_s = np.stack([R[c]["nvs"] for c in range(NCORES)], axis=0).reshape(1, 8, NS, 8, 64)
    nva_s = np.stack([R[c]["nvas"] for c in range(NCORES)], axis=0).reshape(1, 8, NS, 8, 64)
    return (y_prompt.astype(np.float32), y_sample.astype(np.float32), nk_p.astype(np.float32), nv_p.astype(np.float32),
            nk_s.astype(np.float32), nv_s.astype(np.float32), nva_s.astype(np.float32))
```
